# Optimizing a Trainium2 kernel written in Bass

```python
import math
import jax, jax.numpy as jnp
from jax import lax
import numpy as np

D_MODEL = 1024
BATCH = 16
SEQ = 4096
DEPTH = 2

N_HEADS = 16
HEAD_DIM = D_MODEL // N_HEADS
D_FF = ((8 * D_MODEL // 3) + 255) // 256 * 256
CONV_WIDTH = 3
DILATED_BRANCHES = ((128, 1), (512, 4), (2048, 16))
BLOCK = 128
REL_BUCKETS = 32
REL_MAX_DISTANCE = 2048
N_A_LAYERS = DEPTH // 2
N_B_LAYERS = DEPTH - N_A_LAYERS
RMS_EPS = 1e-6

kernel_name = "yoco_shortconv_dilated_attention_trunk"


def rmsnorm(x, g):
    xf = x.astype(jnp.float32)
    y = xf * lax.rsqrt(jnp.mean(xf * xf, axis=-1, keepdims=True) + RMS_EPS)
    return (y * g.astype(jnp.float32)).astype(x.dtype)


def causal_dwconv(x, w, b=None):
    S = x.shape[1]
    xp = jnp.pad(x, ((0, 0), (CONV_WIDTH - 1, 0), (0, 0)))
    y = xp[:, 0:S] * w[0]
    for tap in range(1, CONV_WIDTH):
        y = y + xp[:, tap:tap + S] * w[tap]
    if b is not None:
        y = y + b
    return y


def t5_bucket(dist):
    max_exact = REL_BUCKETS // 2
    n = jnp.maximum(dist, 0)
    nf = jnp.maximum(n, max_exact).astype(jnp.float32)
    large = max_exact + (jnp.log(nf / max_exact) / math.log(REL_MAX_DISTANCE / max_exact)
                         * (REL_BUCKETS - max_exact)).astype(jnp.int32)
    large = jnp.minimum(large, REL_BUCKETS - 1)
    return jnp.where(n < max_exact, n, large)


def short_conv_mixer(xn, w_in, conv_w, w_out):
    b_gate, c_gate, h = jnp.split(xn @ w_in, 3, axis=-1)
    return (b_gate * causal_dwconv(c_gate * h, conv_w)) @ w_out


def conv_ffn(xn, w_up, conv_w, conv_b, w_down):
    u = causal_dwconv(xn @ w_up, conv_w, conv_b)
    g, up = jnp.split(u, 2, axis=-1)
    return (jax.nn.silu(g) * up) @ w_down


def dilated_branch(q, k, v, rel_bias, window, dilation):
    B, S, H, Dh = q.shape
    P = BLOCK
    W = window // dilation
    L = S // dilation
    Lp = -(-L // P) * P
    nb = Lp // P

    def to_sub(t):
        return t.reshape(B, L, dilation, H, Dh).transpose(2, 0, 1, 3, 4)

    def to_blocks(t):
        return (t.reshape(dilation, B, nb, P, H, Dh).transpose(0, 2, 1, 3, 4, 5)
                .reshape(dilation * nb, B, P, H, Dh))

    qs = jnp.pad(to_sub(q), ((0, 0), (0, 0), (0, Lp - L), (0, 0), (0, 0)))
    ks = jnp.pad(to_sub(k), ((0, 0), (0, 0), (P, Lp - L), (0, 0), (0, 0)))
    vs = jnp.pad(to_sub(v), ((0, 0), (0, 0), (P, Lp - L), (0, 0), (0, 0)))
    q_blk = to_blocks(qs)
    k_prev, k_cur = to_blocks(ks[:, :, :Lp]), to_blocks(ks[:, :, P:])
    v_prev, v_cur = to_blocks(vs[:, :, :Lp]), to_blocks(vs[:, :, P:])
    blk_idx = jnp.tile(jnp.arange(nb, dtype=jnp.int32), dilation)

    qi = jnp.arange(P, dtype=jnp.int32)[:, None]
    kc = jnp.arange(2 * P, dtype=jnp.int32)[None, :]
    delta = qi + P - kc
    band = (delta >= 0) & (delta <= W)
    bias = rel_bias[t5_bucket(delta * dilation)].astype(jnp.float32).transpose(2, 0, 1)
    scale = HEAD_DIM ** -0.5

    def block_fn(args):
        qb, kp, kcur, vp, vcur, j = args
        kw = jnp.concatenate([kp, kcur], axis=1).astype(jnp.float32)
        vw = jnp.concatenate([vp, vcur], axis=1).astype(jnp.float32)
        s = jnp.einsum('bqhd,bkhd->bhqk', qb.astype(jnp.float32), kw) * scale + bias
        valid = band & ((j * P + kc - P) >= 0)
        s = jnp.where(valid, s, -jnp.inf)
        m = jnp.max(s, axis=-1)
        p = jnp.exp(s - m[..., None])
        den = jnp.sum(p, axis=-1)
        num = jnp.einsum('bhqk,bkhd->bqhd', p, vw)
        return num, den.transpose(0, 2, 1), m.transpose(0, 2, 1)

    num, den, mx = lax.map(block_fn, (q_blk, k_prev, k_cur, v_prev, v_cur, blk_idx))

    def from_blocks(t):
        t = t.reshape((dilation, nb, B, P) + t.shape[3:])
        t = jnp.moveaxis(jnp.moveaxis(t, 0, 3), 0, 1)
        t = t.reshape((B, Lp, dilation) + t.shape[4:])[:, :L]
        return t.reshape((B, S) + t.shape[3:])

    return from_blocks(num), from_blocks(den), from_blocks(mx)


def dilated_attention(xn, w_q, w_o, k, v, rel_bias):
    B, S, _ = xn.shape
    q = (xn @ w_q).reshape(B, S, N_HEADS, HEAD_DIM)
    branches = [dilated_branch(q, k, v, rel_bias, w, d) for (w, d) in DILATED_BRANCHES]
    m_all = jnp.max(jnp.stack([br[2] for br in branches]), axis=0)
    num_tot = jnp.zeros(q.shape, jnp.float32)
    den_tot = jnp.zeros(m_all.shape, jnp.float32)
    for num, den, mx in branches:
        wgt = jnp.exp(mx - m_all)
        num_tot = num_tot + wgt[..., None] * num
        den_tot = den_tot + wgt * den
    out = (num_tot / den_tot[..., None]).astype(xn.dtype).reshape(B, S, D_MODEL)
    return out @ w_o


def setup_inputs(seed: int = 0) -> dict:
    key = jax.random.key(seed)
    ks = jax.random.split(key, 17)
    f32 = jnp.float32
    D, F = D_MODEL, D_FF

    def nrm(k, shape, scale):
        return jax.random.normal(k, shape, f32) * scale

    def gain(k, shape):
        return 1.0 + 0.02 * jax.random.normal(k, shape, f32)

    return {
        "x": nrm(ks[0], (BATCH, SEQ, D), 1.0),
        "a_norm": gain(ks[1], (N_A_LAYERS, D)),
        "a_w_in": nrm(ks[2], (N_A_LAYERS, D, 3 * D), D ** -0.5),
        "a_conv": nrm(ks[3], (N_A_LAYERS, CONV_WIDTH, D), CONV_WIDTH ** -0.5),
        "a_w_out": nrm(ks[4], (N_A_LAYERS, D, D), D ** -0.5),
        "kv_norm": gain(ks[5], (D,)),
        "w_kv": nrm(ks[6], (D, 2 * D), D ** -0.5),
        "b_norm": gain(ks[7], (N_B_LAYERS, D)),
        "b_w_q": nrm(ks[8], (N_B_LAYERS, D, D), D ** -0.5),
        "b_w_o": nrm(ks[9], (N_B_LAYERS, D, D), D ** -0.5),
        "rel_bias": nrm(ks[10], (REL_BUCKETS, N_HEADS), 0.5),
        "ffn_norm": gain(ks[11], (DEPTH, D)),
        "ffn_w_up": nrm(ks[12], (DEPTH, D, 2 * F), D ** -0.5),
        "ffn_conv": nrm(ks[13], (DEPTH, CONV_WIDTH, 2 * F), CONV_WIDTH ** -0.5),
        "ffn_conv_b": nrm(ks[14], (DEPTH, 2 * F), 0.02),
        "ffn_w_down": nrm(ks[15], (DEPTH, F, D), F ** -0.5),
        "final_norm": gain(ks[16], (D,)),
    }


def reference(x, a_norm, a_w_in, a_conv, a_w_out, kv_norm, w_kv, b_norm, b_w_q, b_w_o, rel_bias,
              ffn_norm, ffn_w_up, ffn_conv, ffn_conv_b, ffn_w_down, final_norm):
    B, S, _ = x.shape
    h = x
    k = v = None
    for l in range(DEPTH):
        if l < N_A_LAYERS:
            h = h + short_conv_mixer(rmsnorm(h, a_norm[l]), a_w_in[l], a_conv[l], a_w_out[l])
        else:
            j = l - N_A_LAYERS
            h = h + dilated_attention(rmsnorm(h, b_norm[j]), b_w_q[j], b_w_o[j], k, v, rel_bias)
        h = h + conv_ffn(rmsnorm(h, ffn_norm[l]), ffn_w_up[l], ffn_conv[l], ffn_conv_b[l], ffn_w_down[l])
        if l == N_A_LAYERS - 1:
            k_flat, v_flat = jnp.split(rmsnorm(h, kv_norm) @ w_kv, 2, axis=-1)
            k = k_flat.reshape(B, S, N_HEADS, HEAD_DIM)
            v = v_flat.reshape(B, S, N_HEADS, HEAD_DIM)
    return rmsnorm(h, final_norm)
```

```python
import math
import numpy as np
import concourse.bass as bass
import concourse.mybir as mybir
from concourse.bass_utils import run_bass_kernel_spmd
from contextlib import ExitStack

F32 = mybir.dt.float32
BF16 = mybir.dt.bfloat16
ALU = mybir.AluOpType
AF = mybir.ActivationFunctionType

D = 1024
FF = 2816
S_LEN = 4096
T = 512
KC = 8
FC = 22
NH = 16
EPS = 1e-6
DILS = (1, 4, 16)
ENGS = ("pe", "act", "dve", "pool", "sp")


class Sched:
    def __init__(self, nc, stack):
        self.nc = nc
        self.stack = stack
        self.ops = {e: [] for e in ENGS}
        self.esem = {e: stack.enter_context(nc.semaphore("prog_" + e)) for e in ENGS}
        self.ecnt = {e: 0 for e in ENGS}
        self.waited = {e: {} for e in ENGS}
        self.res = {}
        self.dsem = {}
        self.dcnt = {}
        self.excl = set()

    def _need(self, eng, toks):
        w = self.waited[eng]
        best = {}
        for t in toks:
            if t is None:
                continue
            key, sem, val, _ = t
            if w.get(key, 0) >= val:
                continue
            if key not in best or best[key][2] < val:
                best[key] = t
        for t in best.values():
            w[t[0]] = t[2]
        return list(best.values())

    def _deps(self, eng, reads, writes):
        toks = []
        for r in reads:
            st = self.res.get(r)
            if st is None:
                continue
            toks.append(st["w"])
            if r in self.excl:
                toks.extend(t for t in st["r"] if t[3] != eng)
        for wkey in writes:
            st = self.res.get(wkey)
            if st is None:
                continue
            tw = st["w"]
            if tw is not None and tw[3] != eng:
                toks.append(tw)
            for tr in st["r"]:
                if tr[3] != eng:
                    toks.append(tr)
        return toks

    def _commit(self, tok, reads, writes):
        for r in reads:
            st = self.res.setdefault(r, {"w": None, "r": []})
            st["r"] = [t for t in st["r"] if t[0] != tok[0]] + [tok]
        for wkey in writes:
            self.res[wkey] = {"w": tok, "r": []}

    def op(self, eng, fn, reads=(), writes=(), signal=True):
        toks = self._need(eng, self._deps(eng, reads, writes))
        n = self.ecnt[eng] + 1
        if signal:
            self.ecnt[eng] = n
        tok = ("E" + eng, self.esem[eng], n, eng)
        self.ops[eng].append((toks, fn, self.esem[eng] if signal else None, 1))
        self._commit(tok, reads, writes)
        return tok

    def dma(self, eng, out, in_, reads=(), writes=(), semkey=None, commit_only=()):
        if semkey is None:
            semkey = (writes[0] if writes else (commit_only[0] if commit_only else reads[0]))
        if semkey not in self.dsem:
            self.dsem[semkey] = self.stack.enter_context(
                self.nc.semaphore("d%d" % len(self.dsem)))
            self.dcnt[semkey] = 0
        toks = self._need(eng, self._deps(eng, reads, writes))
        self.dcnt[semkey] += 16
        sem = self.dsem[semkey]
        tok = ("D" + str(semkey), sem, self.dcnt[semkey], None)

        def fn(e, out=out, in_=in_):
            return e.dma_start(out=out, in_=in_)

        self.ops[eng].append((toks, fn, sem, 16))
        self._commit(tok, reads, list(writes) + list(commit_only))
        return tok

    def barrier(self):
        toks = [("E" + e, self.esem[e], self.ecnt[e], e) for e in ENGS if self.ecnt[e] > 0]
        toks += [("D" + str(k), self.dsem[k], self.dcnt[k], None) for k in self.dsem if self.dcnt[k] > 0]
        for e in ENGS:
            need = self._need(e, [t for t in toks if t[3] != e])
            if need:
                self.ops[e].append((need, None, None, 0))

    def emit(self):
        nc = self.nc
        with nc.Block() as block:
            def run(engobj, name):
                for toks, fn, sem, inc in self.ops[name]:
                    for (_, s, v, _) in toks:
                        engobj.wait_ge(s, v)
                    if fn is None:
                        continue
                    ins = fn(engobj)
                    if sem is not None:
                        ins.then_inc(sem, inc)

            @block.tensor
            def _(e):
                run(e, "pe")

            @block.scalar
            def _(e):
                run(e, "act")

            @block.vector
            def _(e):
                run(e, "dve")

            @block.gpsimd
            def _(e):
                run(e, "pool")

            @block.sync
            def _(e):
                run(e, "sp")


def _t5_bucket(n):
    if n < 16:
        return n
    v = np.log(np.float32(n) / np.float32(16)) / np.float32(math.log(128.0)) * np.float32(16)
    return min(31, 16 + int(np.float32(v)))


def _consts():
    onehot = np.zeros((32, 3 * 384), np.float32)
    band = np.zeros((16, 3 * 384), np.float32)
    for di, d in enumerate(DILS):
        for i in range(384):
            delta = i - 127
            if 0 <= delta <= 128:
                onehot[_t5_bucket(delta * d), di * 384 + i] = 1.0
                band[:, di * 384 + i] = 1.0
    jm = np.zeros((128, 128), np.float32)
    for p in range(128):
        jm[p, 127 - p] = 1.0
    return onehot, band, jm


PC_ANORM, PC_FNORM0, PC_KVNORM, PC_BNORM, PC_FNORM1, PC_FINAL = 0, 8, 16, 24, 32, 40
PC_ACONV = 48
PC_FCONV = 72
PC_FBIAS = 72 + 264
PC_N = PC_FBIAS + 88


def _pack_params(inp):
    cols = []

    def vec(v):
        return np.ascontiguousarray(np.asarray(v, np.float32).reshape(-1, 128).T)

    cols.append(vec(inp["a_norm"][0]))
    cols.append(vec(inp["ffn_norm"][0]))
    cols.append(vec(inp["kv_norm"]))
    cols.append(vec(inp["b_norm"][0]))
    cols.append(vec(inp["ffn_norm"][1]))
    cols.append(vec(inp["final_norm"]))
    for tap in range(3):
        cols.append(vec(inp["a_conv"][0, tap]))
    for l in range(2):
        for tap in range(3):
            cols.append(vec(inp["ffn_conv"][l, tap]))
    for l in range(2):
        cols.append(vec(inp["ffn_conv_b"][l]))
    out = np.concatenate(cols, axis=1)
    assert out.shape == (128, PC_N)
    return np.ascontiguousarray(out)


def build_program(nseq=2, S=S_LEN, stop_after=None):
    NT = S // T
    nc = bass.Bass("TRN2", target_bir_lowering=False)

    def din(name, shape, dt=F32):
        return nc.dram_tensor(name, list(shape), dt, kind="ExternalInput").ap()

    def dscr(name, shape, dt):
        return nc.dram_tensor(name, list(shape), dt, kind="Internal").ap()

    x_fm = din("x_fm", [nseq, D, S])
    a_w_in = din("a_w_in", [D, 3 * D])
    a_w_out = din("a_w_out", [D, D])
    w_kv = din("w_kv", [D, 2 * D])
    b_w_q = din("b_w_q", [D, D])
    b_w_o = din("b_w_o", [D, D])
    w_up = [din("ffn_w_up%d" % l, [D, 2 * FF]) for l in range(2)]
    w_dn = [din("ffn_w_dn%d" % l, [FF, D]) for l in range(2)]
    rel_bias = din("rel_bias", [32, NH])
    prm_in = din("prm", [128, PC_N])
    onehot_in = din("onehot", [32, 1152])
    band_in = din("band", [16, 1152])
    jm_in = din("jm", [128, 128])
    out_fm = nc.dram_tensor("out_fm", [nseq, D, S], F32, kind="ExternalOutput").ap()

    s_win = dscr("s_win", [8, 128, KC * 384], BF16)
    s_wout = dscr("s_wout", [2, 128, KC * 512], BF16)
    s_wkv = dscr("s_wkv", [4, 128, KC * 512], BF16)
    s_wq = dscr("s_wq", [2, 128, KC * 512], BF16)
    s_wo = dscr("s_wo", [2, 128, KC * 512], BF16)
    s_wup = [dscr("s_wup%d" % l, [11, 128, KC * 512], BF16) for l in range(2)]
    s_wdn = [dscr("s_wdn%d" % l, [8, 128, FC * 128], BF16) for l in range(2)]
    s_h1 = dscr("s_h1", [nseq, D, S], F32)
    s_kT = dscr("s_kT", [nseq, D, S], BF16)
    s_qT = dscr("s_qT", [nseq, D, S], BF16)
    s_v = dscr("s_v", [nseq, S, D], BF16)
    s_at = dscr("s_at", [nseq, D, S], BF16)
    s_ev = dscr("s_ev", [16, 1152], F32)

    with ExitStack() as st:
        Sx = Sched(nc, st)
        op, dma = Sx.op, Sx.dma

        def sb(name, shape, dt):
            return st.enter_context(nc.sbuf_tensor(name, list(shape), dt))

        prm = sb("prm_sb", [128, PC_N], F32)
        jm = sb("jm_sb", [128, 128], F32)
        ones_bf = sb("ones_bf", [128, 128], BF16)
        mtab = sb("mtab", [128, 3 * NH * 2 * 128], BF16)
        hal_mix = sb("hal_mix", [128, 8, 2], F32)
        hal_ffn = sb("hal_ffn", [128, 2 * 44, 2], F32)
        NW = 5
        BFA = sb("bfa", [128, 4096 * 3 + 11264 + NW * 4096 + 4096 * 2], BF16)
        F32A = sb("f32a", [128, 4096 + 512 * 2 + 10 * 516], F32)
        psum = st.enter_context(nc.psum_tensor("psum", [128, 8, 512], F32))
        for b in range(8):
            Sx.excl.add(("ps", b))

        o = 0

        def carve(n):
            nonlocal o
            v = BFA[:, o:o + n]
            o += n
            return v

        xn = carve(4096).rearrange("p (a b) -> p a b", a=KC)
        yb = carve(4096).rearrange("p (a b) -> p a b", a=KC)
        sqb = carve(4096)
        ab = carve(11264).rearrange("p (a b) -> p a b", a=FC)
        wslots = [carve(4096) for _ in range(NW)]
        kts = carve(4096).rearrange("p (a b) -> p a b", a=KC)
        vs = carve(4096).rearrange("p (a b) -> p a b", a=4)
        h = F32A[:, 0:4096].rearrange("p (a b) -> p a b", a=KC)
        rstd_i = F32A[:, 4096:4608]
        rstd = F32A[:, 4608:5120]
        wk = [F32A[:, 5120 + i * 516: 5120 + (i + 1) * 516] for i in range(10)]

        qc = BFA[:, 0:2048]
        kcb = BFA[:, 2048:2048 + 4096]
        vc = BFA[:, 6144:6144 + 3 * 32 * 128].rearrange("p (l b f) -> p l b f", l=3, b=32)
        pT = [BFA[:, 18432 + i * 512: 18432 + (i + 1) * 512] for i in range(4)]
        ato = BFA[:, 20480:20480 + 2048]
        accn = F32A[:, 0:2048]
        accd = F32A[:, 2048:4096]
        rec = F32A[:, 5120:5120 + 2048]

        pctr = [0]

        def newbank():
            b = pctr[0] % 8
            pctr[0] += 1
            return b

        wctr = [0]

        def load_w(src, n, mkey):
            s = wctr[0] % NW
            wctr[0] += 1
            dma("sp", wslots[s][:, 0:n], src, reads=[mkey], writes=[("w", s)])
            return s

        dma("sp", prm[:], prm_in, writes=["prm"])
        dma("sp", jm[:], jm_in, writes=["jm"])
        op("dve", lambda e: e.memset(ones_bf[:], 1.0), writes=["ones"])
        op("dve", lambda e: e.memset(hal_mix[:], 0.0), writes=["hal_mix"])
        op("dve", lambda e: e.memset(hal_ffn[:], 0.0), writes=["hal_ffn"])

        def cast(dst, src, key):
            dma("pool", dst, src, commit_only=[key], semkey=key)

        for j in range(8):
            for t in range(3):
                cast(s_win[j].rearrange("p (kc t m) -> p kc t m", kc=KC, t=3)[:, :, t, :],
                     a_w_in.rearrange("(kc p) (t j m) -> j t p kc m", p=128, t=3, j=8)[j, t], "m_win")
        for b in range(2):
            cast(s_wout[b].rearrange("p (kc m) -> p kc m", kc=KC),
                 a_w_out.rearrange("(kc p) (b m) -> b p kc m", p=128, m=512)[b], "m_wout")

        def cast_ffn(l):
            for jb in range(11):
                for jj in range(2):
                    for half in range(2):
                        c0 = half * FF + (2 * jb + jj) * 128
                        cast(s_wup[l][jb].rearrange("p (kc s m) -> p kc s m", kc=KC, s=4)[:, :, jj * 2 + half, :],
                             w_up[l][:, c0:c0 + 128].rearrange("(kc p) m -> p kc m", p=128), "m_wup%d" % l)
            for j in range(8):
                cast(s_wdn[l][j].rearrange("p (kc m) -> p kc m", kc=FC),
                     w_dn[l].rearrange("(kc p) (j m) -> j p kc m", p=128, m=128)[j], "m_wdn%d" % l)

        cast_ffn(0)
        for b in range(4):
            cast(s_wkv[b].rearrange("p (kc m) -> p kc m", kc=KC),
                 w_kv.rearrange("(kc p) (b m) -> b p kc m", p=128, m=512)[b], "m_wkv")
        for b in range(2):
            cast(s_wq[b].rearrange("p (kc m) -> p kc m", kc=KC),
                 b_w_q.rearrange("(kc p) (b m) -> b p kc m", p=128, m=512)[b], "m_wq")
        for b in range(2):
            cast(s_wo[b].rearrange("p (kc m) -> p kc m", kc=KC),
                 b_w_o.rearrange("(kc p) (b m) -> b p kc m", p=128, m=512)[b], "m_wo")
        cast_ffn(1)

        def build_tables():
            rb = F32A[0:32, 0:16]
            oh = F32A[0:32, 16:16 + 1152]
            bd = F32A[0:16, 1200:1200 + 1152]
            ev = F32A[0:16, 2400:2400 + 1152]
            dma("sp", rb, rel_bias, writes=["tb_rb"])
            dma("sp", oh, onehot_in, writes=["tb_oh"])
            dma("sp", bd, band_in, writes=["tb_bd"])
            for di in range(3):
                b = newbank()
                op("pe", lambda e, b=b, di=di: e.matmul(psum[0:16, b, 0:384], lhsT=rb, rhs=oh[:, di * 384:(di + 1) * 384],
                                                       start=True, stop=True),
                   reads=["tb_rb", "tb_oh"], writes=[("ps", b)])
                op("act", lambda e, b=b, di=di: e.activation(out=ev[:, di * 384:(di + 1) * 384], in_=psum[0:16, b, 0:384], func=AF.Exp),
                   reads=[("ps", b)], writes=["tb_ev"])
            op("dve", lambda e: e.tensor_tensor(out=ev, in0=ev, in1=bd, op=ALU.mult), reads=["tb_ev", "tb_bd"], writes=["tb_ev"])
            dma("sp", s_ev, ev, reads=["tb_ev"], writes=["s_ev"])
            hk = F32A[:, 4096:4096 + 4096]
            mt = mtab[:].rearrange("p (d n) -> p d n", d=3)
            for di in range(3):
                src = bass.AP(tensor=s_ev.tensor, offset=di * 384, ap=[[1, 128], [1152, 16], [128, 2], [1, 128]])
                dma("sp", hk.rearrange("p (h k q) -> p h k q", h=16, k=2), src, reads=["s_ev"], writes=["tb_hk"])
                for c8 in range(8):
                    b = newbank()
                    op("pe", lambda e, b=b, c8=c8: e.matmul(psum[:, b, :], lhsT=jm[:], rhs=hk[:, c8 * 512:(c8 + 1) * 512],
                                                           start=True, stop=True),
                       reads=["jm", "tb_hk"], writes=[("ps", b)])
                    op("act", lambda e, b=b, c8=c8, di=di: e.copy(out=mt[:, di, c8 * 512:(c8 + 1) * 512], in_=psum[:, b, :]),
                       reads=[("ps", b)], writes=["mtab"])

        build_tables()
        Sx.barrier()

        def rmsnorm(pcol, out_bf=True):
            op("act", lambda e: e.activation(out=sqb, in_=h.rearrange("p a b -> p (a b)"), func=AF.Square),
               reads=["h"], writes=["sqb"])
            b = newbank()
            for kc in range(KC):
                op("pe", lambda e, kc=kc, b=b: e.matmul(psum[:, b, :], lhsT=ones_bf[:], rhs=sqb[:, kc * 512:(kc + 1) * 512],
                                                       start=(kc == 0), stop=(kc == KC - 1)),
                   reads=["ones", "sqb"], writes=[("ps", b)], signal=(kc == KC - 1))
            op("act", lambda e, b=b: e.activation(out=rstd_i, in_=psum[:, b, :], func=AF.Sqrt, scale=1.0 / D, bias=EPS),
               reads=[("ps", b)], writes=["rstd_i"])
            op("dve", lambda e: e.reciprocal(out=rstd, in_=rstd_i), reads=["rstd_i"], writes=["rstd"])
            for kc in range(KC):
                if out_bf:
                    op("dve", lambda e, kc=kc: e.scalar_tensor_tensor(out=xn[:, kc, :], in0=h[:, kc, :], scalar=prm[:, pcol + kc:pcol + kc + 1],
                                                                       in1=rstd, op0=ALU.mult, op1=ALU.mult),
                       reads=["h", "rstd", "prm"], writes=["xn"])
                else:
                    op("dve", lambda e, kc=kc: e.scalar_tensor_tensor(out=h[:, kc, :], in0=h[:, kc, :], scalar=prm[:, pcol + kc:pcol + kc + 1],
                                                                       in1=rstd, op0=ALU.mult, op1=ALU.mult),
                       reads=["h", "rstd", "prm"], writes=["h"])

        def mm_group(b, wslot, woff, wstride, rhs_fn, nk, rkeys):
            for kc in range(nk):
                op("pe", lambda e, kc=kc: e.matmul(psum[:, b, :], lhsT=wslots[wslot][:, kc * wstride + woff: kc * wstride + woff + 128],
                                                  rhs=rhs_fn(kc), start=(kc == 0), stop=(kc == nk - 1)),
                   reads=[("w", wslot)] + rkeys, writes=[("ps", b)], signal=(kc == nk - 1))

        def conv_taps(acc, u, c0, c1, c2, bias=None):
            if bias is None:
                op("dve", lambda e: e.tensor_scalar(out=acc[:, 0:512], in0=u[:, 2:514], scalar1=prm[:, c2:c2 + 1], scalar2=None, op0=ALU.mult),
                   reads=[u_key(u), "prm"], writes=[u_key(acc)])
            else:
                op("dve", lambda e: e.tensor_scalar(out=acc[:, 0:512], in0=u[:, 2:514], scalar1=prm[:, c2:c2 + 1], scalar2=prm[:, bias:bias + 1],
                                                    op0=ALU.mult, op1=ALU.add),
                   reads=[u_key(u), "prm"], writes=[u_key(acc)])
            op("dve", lambda e: e.scalar_tensor_tensor(out=acc[:, 0:512], in0=u[:, 1:513], scalar=prm[:, c1:c1 + 1], in1=acc[:, 0:512],
                                                       op0=ALU.mult, op1=ALU.add),
               reads=[u_key(u), u_key(acc), "prm"], writes=[u_key(acc)])
            op("dve", lambda e: e.scalar_tensor_tensor(out=acc[:, 0:512], in0=u[:, 0:512], scalar=prm[:, c0:c0 + 1], in1=acc[:, 0:512],
                                                       op0=ALU.mult, op1=ALU.add),
               reads=[u_key(u), u_key(acc), "prm"], writes=[u_key(acc)])

        wk_ids = {}

        def u_key(ap):
            return ("wk", wk_ids[id(ap)])

        for i, w_ in enumerate(wk):
            wk_ids[id(w_)] = i

        def halo_io(u, hal_ap, hkey):
            op("pool", lambda e: e.tensor_copy(out=u[:, 0:2], in_=hal_ap), reads=[hkey], writes=[u_key(u)])
            op("pool", lambda e: e.tensor_copy(out=hal_ap, in_=u[:, 512:514]), reads=[u_key(u)], writes=[hkey])

        def residual_proj(s_w, mkey, rhs_buf, rkey, nk, per_block):
            if per_block == 4:
                for blk in range(2):
                    ws = load_w(s_w[blk], 4096, mkey)
                    for m in range(4):
                        j = blk * 4 + m
                        b = newbank()
                        mm_group(b, ws, m * 128, 512, lambda kc: rhs_buf[:, kc, :], nk, [rkey])
                        op("dve", lambda e, b=b, j=j: e.tensor_tensor(out=h[:, j, :], in0=psum[:, b, :], in1=h[:, j, :], op=ALU.add),
                           reads=[("ps", b), "h"], writes=["h"])
            else:
                for j in range(8):
                    ws = load_w(s_w[j], FC * 128, mkey)
                    b = newbank()
                    mm_group(b, ws, 0, 128, lambda kc: rhs_buf[:, kc, :], nk, [rkey])
                    op("dve", lambda e, b=b, j=j: e.tensor_tensor(out=h[:, j, :], in0=psum[:, b, :], in1=h[:, j, :], op=ALU.add),
                       reads=[("ps", b), "h"], writes=["h"])

        def ffn(l, pnorm):
            rmsnorm(pnorm)
            for jb in range(11):
                ws = load_w(s_wup[l][jb], 4096, "m_wup%d" % l)
                for jj in range(2):
                    j = 2 * jb + jj
                    par = j % 2
                    ug, uu, ag, au, sg = [wk[par * 5 + i] for i in range(5)]
                    bg, bu = newbank(), newbank()
                    mm_group(bg, ws, (jj * 2 + 0) * 128, 512, lambda kc: xn[:, kc, :], KC, ["xn"])
                    mm_group(bu, ws, (jj * 2 + 1) * 128, 512, lambda kc: xn[:, kc, :], KC, ["xn"])
                    for (bb, u, cidx) in ((bg, ug, j), (bu, uu, 22 + j)):
                        hal_ap = hal_ffn[:, l * 44 + cidx, :]
                        hkey = ("hf", l, cidx)
                        op("pool", lambda e, u=u, hal_ap=hal_ap: e.tensor_copy(out=u[:, 0:2], in_=hal_ap), reads=[hkey, "hal_ffn"], writes=[u_key(u)])
                        op("act", lambda e, u=u, bb=bb: e.copy(out=u[:, 2:514], in_=psum[:, bb, :]), reads=[("ps", bb)], writes=[u_key(u)])
                        op("pool", lambda e, u=u, hal_ap=hal_ap: e.tensor_copy(out=hal_ap, in_=u[:, 512:514]), reads=[u_key(u)], writes=[hkey])
                    base = PC_FCONV + l * 132
                    conv_taps(ag, ug, base + j, base + 44 + j, base + 88 + j, bias=PC_FBIAS + l * 44 + j)
                    conv_taps(au, uu, base + 22 + j, base + 44 + 22 + j, base + 88 + 22 + j, bias=PC_FBIAS + l * 44 + 22 + j)
                    op("act", lambda e, sg=sg, ag=ag: e.activation(out=sg[:, 0:512], in_=ag[:, 0:512], func=AF.Silu),
                       reads=[u_key(ag)], writes=[u_key(sg)])
                    op("pool", lambda e, sg=sg, au=au, j=j: e.tensor_tensor(out=ab[:, j, :], in0=sg[:, 0:512], in1=au[:, 0:512], op=ALU.mult),
                       reads=[u_key(sg), u_key(au)], writes=["ab"])
            residual_proj(s_wdn[l], "m_wdn%d" % l, ab, "ab", FC, 1)

        def tile_cols(ap3, t):
            return ap3[:, :, t * T:(t + 1) * T]

        def fm_view(d2):
            return d2.rearrange("(kc p) s -> p kc s", p=128)

        def phase_a(sq):
            op("dve", lambda e: e.memset(hal_mix[:], 0.0), writes=["hal_mix"] + [("hm", j) for j in range(8)])
            op("dve", lambda e: e.memset(hal_ffn[:], 0.0), writes=["hal_ffn"] + [("hf", 0, c) for c in range(44)])
            for t in range(NT):
                dma("sp", h, tile_cols(fm_view(x_fm[sq]), t), writes=["h"])
                rmsnorm(PC_ANORM)
                for j in range(8):
                    ws = load_w(s_win[j], KC * 384, "m_win")
                    bb_, bc_, bh_ = newbank(), newbank(), newbank()
                    mm_group(bb_, ws, 0, 384, lambda kc: xn[:, kc, :], KC, ["xn"])
                    mm_group(bc_, ws, 128, 384, lambda kc: xn[:, kc, :], KC, ["xn"])
                    mm_group(bh_, ws, 256, 384, lambda kc: xn[:, kc, :], KC, ["xn"])
                    par = j % 2
                    ch, hs, acc = wk[par * 5 + 0], wk[par * 5 + 1], wk[par * 5 + 2]
                    hal_ap = hal_mix[:, j, :]
                    hkey = ("hm", j)
                    op("act", lambda e, hs=hs, bh_=bh_: e.copy(out=hs[:, 0:512], in_=psum[:, bh_, :]), reads=[("ps", bh_)], writes=[u_key(hs)])
                    op("pool", lambda e, ch=ch, hal_ap=hal_ap: e.tensor_copy(out=ch[:, 0:2], in_=hal_ap), reads=[hkey, "hal_mix"], writes=[u_key(ch)])
                    op("dve", lambda e, ch=ch, hs=hs, bc_=bc_: e.tensor_tensor(out=ch[:, 2:514], in0=psum[:, bc_, :], in1=hs[:, 0:512], op=ALU.mult),
                       reads=[("ps", bc_), u_key(hs)], writes=[u_key(ch)])
                    op("pool", lambda e, ch=ch, hal_ap=hal_ap: e.tensor_copy(out=hal_ap, in_=ch[:, 512:514]), reads=[u_key(ch)], writes=[hkey])
                    conv_taps(acc, ch, PC_ACONV + j, PC_ACONV + 8 + j, PC_ACONV + 16 + j)
                    op("dve", lambda e, acc=acc, bb_=bb_, j=j: e.tensor_tensor(out=yb[:, j, :], in0=psum[:, bb_, :], in1=acc[:, 0:512], op=ALU.mult),
                       reads=[("ps", bb_), u_key(acc)], writes=["yb"])
                residual_proj(s_wout, "m_wout", yb, "yb", KC, 4)
                ffn(0, PC_FNORM0)
                rmsnorm(PC_KVNORM)
                for blk in range(2):
                    ws = load_w(s_wkv[blk], 4096, "m_wkv")
                    for m in range(4):
                        j = blk * 4 + m
                        b = newbank()
                        mm_group(b, ws, m * 128, 512, lambda kc: xn[:, kc, :], KC, ["xn"])
                        op("act", lambda e, b=b, j=j: e.copy(out=kts[:, j, :], in_=psum[:, b, :]), reads=[("ps", b)], writes=["kts"])
                dma("pool", tile_cols(fm_view(s_kT[sq]), t), kts, reads=["kts"], commit_only=[("s_kT", sq)], semkey="st_kts")
                for half in range(2):
                    ws = load_w(s_wkv[2 + half], 4096, "m_wkv")
                    for blk in range(4):
                        b = newbank()
                        for kc in range(KC):
                            op("pe", lambda e, kc=kc, b=b, blk=blk, ws=ws: e.matmul(psum[:, b, :], lhsT=xn[:, kc, blk * 128:(blk + 1) * 128],
                                                                                  rhs=wslots[ws][:, kc * 512:(kc + 1) * 512],
                                                                                  start=(kc == 0), stop=(kc == KC - 1)),
                               reads=[("w", ws), "xn"], writes=[("ps", b)], signal=(kc == KC - 1))
                        op("act", lambda e, b=b, blk=blk, half=half: e.copy(out=vs[:, blk, half * 512:(half + 1) * 512], in_=psum[:, b, :]),
                           reads=[("ps", b)], writes=["vs"])
                dma("pool", s_v[sq, t * T:(t + 1) * T, :].rearrange("(b p) f -> p b f", p=128), vs, reads=["vs"],
                    commit_only=[("s_v", sq)], semkey="st_vs")
                rmsnorm(PC_BNORM)
                for blk in range(2):
                    ws = load_w(s_wq[blk], 4096, "m_wq")
                    for m in range(4):
                        j = blk * 4 + m
                        b = newbank()
                        mm_group(b, ws, m * 128, 512, lambda kc: xn[:, kc, :], KC, ["xn"])
                        op("act", lambda e, b=b, j=j: e.copy(out=yb[:, j, :], in_=psum[:, b, :]), reads=[("ps", b)], writes=["yb"])
                dma("pool", tile_cols(fm_view(s_qT[sq]), t), yb, reads=["yb"], commit_only=[("s_qT", sq)], semkey="st_q")
                dma("pool", tile_cols(fm_view(s_h1[sq]), t), h, reads=["h"], commit_only=[("s_h1", sq)], semkey="st_h1")

        mt4 = mtab[:].rearrange("p (d h n) -> p d h n", d=3, h=NH)

        def phase_b2(sq):
            NHF = S // 2048
            for c in range(8):
                dma("sp", kcb[:, 0:S], s_kT[sq, c * 128:(c + 1) * 128, :], reads=[("s_kT", sq)], writes=["kcb"])
                vsrc = s_v[sq, :, c * 128:(c + 1) * 128]
                dma("sp", vc[:, 0, 0:S // 128, :], vsrc.rearrange("(b p) f -> p b f", p=128), reads=[("s_v", sq)], writes=["vc0"])
                for jj in range(S // 512):
                    dma("sp", vc[:, 1, jj * 4:(jj + 1) * 4, :], vsrc[jj * 512:(jj + 1) * 512, :].rearrange("(p r) f -> p r f", r=4),
                        reads=[("s_v", sq)], writes=[("vc1", jj)])
                for jj in range(NHF):
                    for rg in range(4):
                        dma("sp", vc[:, 2, jj * 16 + rg * 4: jj * 16 + rg * 4 + 4, :],
                            vsrc[jj * 2048:(jj + 1) * 2048, :].rearrange("(p r) f -> p r f", r=16)[:, rg * 4:(rg + 1) * 4, :],
                            reads=[("s_v", sq)], writes=[("vc2", jj, rg)])
                vkeys = ["vc0"] + [("vc1", jj) for jj in range(S // 512)] + [("vc2", jj, rg) for jj in range(NHF) for rg in range(4)]
                for hf in range(NHF):
                    dma("sp", qc, s_qT[sq, c * 128:(c + 1) * 128, hf * 2048:(hf + 1) * 2048], reads=[("s_qT", sq)], writes=["qc"])
                    for di, d in enumerate(DILS):
                        for g in range(4):
                            bn, bd_ = newbank(), newbank()
                            first = {0: True, 1: True}
                            for pr in range(2):
                                blks = []
                                for bi in range(2):
                                    u = g * 4 + pr * 2 + bi
                                    if d == 1:
                                        jb = hf * 16 + u
                                        qsl = slice(u * 128, (u + 1) * 128)
                                        kcur = slice(jb * 128, (jb + 1) * 128)
                                        jp = max(jb - 1, 0)
                                        kprev = slice(jp * 128, (jp + 1) * 128)
                                        vcur, vprev, has_prev = jb, jp, jb >= 1
                                    elif d == 4:
                                        lt, r = u // 4, u % 4
                                        j4 = hf * 4 + lt
                                        qsl = slice(lt * 512 + r, (lt + 1) * 512, 4)
                                        kcur = slice(j4 * 512 + r, (j4 + 1) * 512, 4)
                                        jp = max(j4 - 1, 0)
                                        kprev = slice(jp * 512 + r, (jp + 1) * 512, 4)
                                        vcur, vprev, has_prev = j4 * 4 + r, jp * 4 + r, j4 >= 1
                                    else:
                                        r = u
                                        qsl = slice(r, 2048, 16)
                                        kcur = slice(hf * 2048 + r, (hf + 1) * 2048, 16)
                                        jp = max(hf - 1, 0)
                                        kprev = slice(jp * 2048 + r, (jp + 1) * 2048, 16)
                                        vcur, vprev, has_prev = hf * 16 + r, jp * 16 + r, hf >= 1
                                    blks.append((qsl, kcur, kprev, vcur, vprev, has_prev))
                                bs = [newbank(), newbank()]
                                for bi, (qsl, kcur, kprev, vcur, vprev, has_prev) in enumerate(blks):
                                    for kbp, ksl in enumerate((kcur, kprev)):
                                        for hd in range(2):
                                            last = (bi == 1 and kbp == 1)
                                            op("pe", lambda e, hd=hd, bi=bi, kbp=kbp, ksl=ksl, qsl=qsl, bs=bs: e.matmul(
                                                psum[:, bs[hd], (bi * 2 + kbp) * 128:(bi * 2 + kbp + 1) * 128],
                                                lhsT=kcb[hd * 64:(hd + 1) * 64, ksl], rhs=qc[hd * 64:(hd + 1) * 64, qsl],
                                                start=True, stop=True),
                                               reads=["kcb", "qc"], writes=[("ps", bs[hd])], signal=last)
                                pts = []
                                for hd in range(2):
                                    pt = pT[(pr * 2 + hd) % 4]
                                    pkey = ("pT", (pr * 2 + hd) % 4)
                                    pts.append((pt, pkey))
                                    op("act", lambda e, pt=pt, hd=hd, bs=bs: e.activation(out=pt, in_=psum[:, bs[hd], :], func=AF.Exp, scale=0.125),
                                       reads=[("ps", bs[hd])], writes=[pkey])
                                    hh = c * 2 + hd
                                    for bi in range(2):
                                        op("dve", lambda e, pt=pt, bi=bi, hh=hh, di=di: e.tensor_tensor(
                                            out=pt[:, bi * 256:(bi + 1) * 256], in0=pt[:, bi * 256:(bi + 1) * 256],
                                            in1=mt4[:, di, hh, :], op=ALU.mult),
                                           reads=[pkey, "mtab"], writes=[pkey])
                                for bi, (qsl, kcur, kprev, vcur, vprev, has_prev) in enumerate(blks):
                                    col = (pr * 2 + bi) * 128
                                    for hd in range(2):
                                        pt, pkey = pts[hd]
                                        for kbp, vb in enumerate((vcur, vprev)):
                                            if kbp == 1 and not has_prev:
                                                continue
                                            rhs = pt[:, (bi * 2 + kbp) * 128:(bi * 2 + kbp + 1) * 128]
                                            stf = first[hd]
                                            first[hd] = False
                                            op("pe", lambda e, hd=hd, vb=vb, rhs=rhs, col=col, stf=stf, di=di, bn=bn: e.matmul(
                                                psum[hd * 64:(hd + 1) * 64, bn, col:col + 128],
                                                lhsT=vc[:, di, vb, hd * 64:(hd + 1) * 64], rhs=rhs, start=stf, stop=True,
                                                skip_group_check=True),
                                               reads=[pkey] + vkeys, writes=[("ps", bn)], signal=False)
                                            op("pe", lambda e, hd=hd, rhs=rhs, col=col, stf=stf, bd_=bd_: e.matmul(
                                                psum[hd * 64:(hd + 1) * 64, bd_, col:col + 128],
                                                lhsT=ones_bf[:, 0:64], rhs=rhs, start=stf, stop=True,
                                                skip_group_check=True),
                                               reads=[pkey, "ones"], writes=[("ps", bd_)], signal=True)
                            for (bk, acc, akey) in ((bn, accn, "accn"), (bd_, accd, "accd")):
                                if d == 1:
                                    oap = acc[:, g * 512:(g + 1) * 512]
                                    iap = psum[:, bk, :]
                                elif d == 4:
                                    oap = acc[:, g * 512:(g + 1) * 512].rearrange("f (p r) -> f r p", r=4)
                                    iap = psum[:, bk, :].rearrange("f (r p) -> f r p", r=4)
                                else:
                                    oap = acc.rearrange("f (p r) -> f r p", r=16)[:, g * 4:(g + 1) * 4, :]
                                    iap = psum[:, bk, :].rearrange("f (r p) -> f r p", r=4)
                                if d == 1:
                                    op("act", lambda e, oap=oap, iap=iap: e.copy(out=oap, in_=iap), reads=[("ps", bk)], writes=[akey])
                                else:
                                    op("dve", lambda e, oap=oap, iap=iap: e.tensor_tensor(out=oap, in0=iap, in1=oap, op=ALU.add),
                                       reads=[("ps", bk), akey], writes=[akey])
                    op("dve", lambda e: e.reciprocal(out=rec, in_=accd), reads=["accd"], writes=["rec"])
                    op("dve", lambda e: e.tensor_tensor(out=ato, in0=accn, in1=rec, op=ALU.mult), reads=["accn", "rec"], writes=["ato"])
                    dma("pool", s_at[sq, c * 128:(c + 1) * 128, hf * 2048:(hf + 1) * 2048], ato, reads=["ato"],
                        commit_only=[("s_at", sq)], semkey="st_ato")

        def phase_b3(sq):
            op("dve", lambda e: e.memset(hal_ffn[:], 0.0), writes=["hal_ffn"] + [("hf", 1, c) for c in range(44)])
            for t in range(NT):
                dma("sp", h, tile_cols(fm_view(s_h1[sq]), t), reads=[("s_h1", sq)], writes=["h"])
                dma("sp", yb, tile_cols(fm_view(s_at[sq]), t), reads=[("s_at", sq)], writes=["yb"])
                residual_proj(s_wo, "m_wo", yb, "yb", KC, 4)
                ffn(1, PC_FNORM1)
                rmsnorm(PC_FINAL, out_bf=False)
                dma("pool", tile_cols(fm_view(out_fm[sq]), t), h, reads=["h"], commit_only=["out"], semkey="st_out")

        for sq in range(nseq):
            if stop_after == "tables":
                break
            phase_a(sq)
            Sx.barrier()
            if stop_after == "a":
                break
            phase_b2(sq)
            Sx.barrier()
            if stop_after == "b2":
                break
            phase_b3(sq)
            Sx.barrier()
        Sx.barrier()
        Sx.emit()
    return nc


_NC_CACHE = {}


def kernel(**inputs):
    x = np.asarray(inputs["x"], np.float32)
    B = x.shape[0]
    ncores = 8
    nseq = B // ncores
    if "nc" not in _NC_CACHE:
        _NC_CACHE["nc"] = build_program(nseq=nseq)
    nc = _NC_CACHE["nc"]
    onehot, band, jm = _consts()
    prm = _pack_params(inputs)
    f32 = lambda a: np.ascontiguousarray(np.asarray(a, np.float32))
    shared = {
        "a_w_in": f32(inputs["a_w_in"][0]),
        "a_w_out": f32(inputs["a_w_out"][0]),
        "w_kv": f32(inputs["w_kv"]),
        "b_w_q": f32(inputs["b_w_q"][0]),
        "b_w_o": f32(inputs["b_w_o"][0]),
        "ffn_w_up0": f32(inputs["ffn_w_up"][0]),
        "ffn_w_up1": f32(inputs["ffn_w_up"][1]),
        "ffn_w_dn0": f32(inputs["ffn_w_down"][0]),
        "ffn_w_dn1": f32(inputs["ffn_w_down"][1]),
        "rel_bias": f32(inputs["rel_bias"]),
        "prm": prm, "onehot": onehot, "band": band, "jm": jm,
    }
    in_maps = []
    for c in range(ncores):
        m = dict(shared)
        m["x_fm"] = np.ascontiguousarray(x[c * nseq:(c + 1) * nseq].transpose(0, 2, 1))
        in_maps.append(m)
    res = run_bass_kernel_spmd(nc, in_maps, core_ids=list(range(ncores)))
    outs = [np.asarray(r["out_fm"]).transpose(0, 2, 1) for r in res.results]
    return np.ascontiguousarray(np.concatenate(outs, axis=0).astype(np.float32))
```

```python
import math
import numpy as np
import concourse.bass as bass
import concourse.mybir as mybir
from concourse.bass_utils import run_bass_kernel_spmd
from contextlib import ExitStack

F32 = mybir.dt.float32
BF16 = mybir.dt.bfloat16
ALU = mybir.AluOpType
AF = mybir.ActivationFunctionType

D = 1024
FF = 2816
S_LEN = 4096
T = 512
KC = 8
FC = 22
NH = 16
EPS = 1e-6
DILS = (1, 4, 16)
ENGS = ("pe", "act", "dve", "pool", "sp")


class Sched:
    def __init__(self, nc, stack):
        self.nc = nc
        self.stack = stack
        self.ops = {e: [] for e in ENGS}
        self.esem = {e: stack.enter_context(nc.semaphore("prog_" + e)) for e in ENGS}
        self.ecnt = {e: 0 for e in ENGS}
        self.waited = {e: {} for e in ENGS}
        self.res = {}
        self.dsem = {}
        self.dcnt = {}
        self.excl = set()

    def _need(self, eng, toks):
        w = self.waited[eng]
        best = {}
        for t in toks:
            if t is None:
                continue
            key, sem, val, _ = t
            if w.get(key, 0) >= val:
                continue
            if key not in best or best[key][2] < val:
                best[key] = t
        for t in best.values():
            w[t[0]] = t[2]
        return list(best.values())

    def _deps(self, eng, reads, writes):
        toks = []
        for r in reads:
            st = self.res.get(r)
            if st is None:
                continue
            toks.append(st["w"])
            if r in self.excl:
                toks.extend(t for t in st["r"] if t[3] != eng)
        for wkey in writes:
            st = self.res.get(wkey)
            if st is None:
                continue
            tw = st["w"]
            if tw is not None and tw[3] != eng:
                toks.append(tw)
            for tr in st["r"]:
                if tr[3] != eng:
                    toks.append(tr)
        return toks

    def _commit(self, tok, reads, writes):
        for r in reads:
            st = self.res.setdefault(r, {"w": None, "r": []})
            st["r"] = [t for t in st["r"] if t[0] != tok[0]] + [tok]
        for wkey in writes:
            self.res[wkey] = {"w": tok, "r": []}

    def op(self, eng, fn, reads=(), writes=(), signal=True):
        toks = self._need(eng, self._deps(eng, reads, writes))
        n = self.ecnt[eng] + 1
        if signal:
            self.ecnt[eng] = n
        tok = ("E" + eng, self.esem[eng], n, eng)
        self.ops[eng].append((toks, fn, self.esem[eng] if signal else None, 1))
        self._commit(tok, reads, writes)
        return tok

    def dma(self, eng, out, in_, reads=(), writes=(), semkey=None, commit_only=()):
        if semkey is None:
            semkey = (writes[0] if writes else (commit_only[0] if commit_only else reads[0]))
        if semkey not in self.dsem:
            self.dsem[semkey] = self.stack.enter_context(
                self.nc.semaphore("d%d" % len(self.dsem)))
            self.dcnt[semkey] = 0
        toks = self._need(eng, self._deps(eng, reads, writes))
        self.dcnt[semkey] += 16
        sem = self.dsem[semkey]
        tok = ("D" + str(semkey), sem, self.dcnt[semkey], None)

        def fn(e, out=out, in_=in_):
            return e.dma_start(out=out, in_=in_)

        self.ops[eng].append((toks, fn, sem, 16))
        self._commit(tok, reads, list(writes) + list(commit_only))
        return tok

    def barrier(self):
        toks = [("E" + e, self.esem[e], self.ecnt[e], e) for e in ENGS if self.ecnt[e] > 0]
        toks += [("D" + str(k), self.dsem[k], self.dcnt[k], None) for k in self.dsem if self.dcnt[k] > 0]
        for e in ENGS:
            need = self._need(e, [t for t in toks if t[3] != e])
            if need:
                self.ops[e].append((need, None, None, 0))

    def emit(self):
        nc = self.nc
        with nc.Block() as block:
            def run(engobj, name):
                for toks, fn, sem, inc in self.ops[name]:
                    for (_, s, v, _) in toks:
                        engobj.wait_ge(s, v)
                    if fn is None:
                        continue
                    ins = fn(engobj)
                    if sem is not None:
                        ins.then_inc(sem, inc)

            @block.tensor
            def _(e):
                run(e, "pe")

            @block.scalar
            def _(e):
                run(e, "act")

            @block.vector
            def _(e):
                run(e, "dve")

            @block.gpsimd
            def _(e):
                run(e, "pool")

            @block.sync
            def _(e):
                run(e, "sp")


def _t5_bucket(n):
    if n < 16:
        return n
    v = np.log(np.float32(n) / np.float32(16)) / np.float32(math.log(128.0)) * np.float32(16)
    return min(31, 16 + int(np.float32(v)))


def _consts():
    onehot = np.zeros((32, 3 * 384), np.float32)
    band = np.zeros((16, 3 * 384), np.float32)
    for di, d in enumerate(DILS):
        for i in range(384):
            delta = i - 127
            if 0 <= delta <= 128:
                onehot[_t5_bucket(delta * d), di * 384 + i] = 1.0
                band[:, di * 384 + i] = 1.0
    jm = np.zeros((128, 128), np.float32)
    for p in range(128):
        jm[p, 127 - p] = 1.0
    return onehot, band, jm


PC_ANORM, PC_FNORM0, PC_KVNORM, PC_BNORM, PC_FNORM1, PC_FINAL = 0, 8, 16, 24, 32, 40
PC_ACONV = 48
PC_FCONV = 72
PC_FBIAS = 72 + 264
PC_N = PC_FBIAS + 88


def _pack_params(inp):
    cols = []

    def vec(v):
        return np.ascontiguousarray(np.asarray(v, np.float32).reshape(-1, 128).T)

    cols.append(vec(inp["a_norm"][0]))
    cols.append(vec(inp["ffn_norm"][0]))
    cols.append(vec(inp["kv_norm"]))
    cols.append(vec(inp["b_norm"][0]))
    cols.append(vec(inp["ffn_norm"][1]))
    cols.append(vec(inp["final_norm"]))
    for tap in range(3):
        cols.append(vec(inp["a_conv"][0, tap]))
    for l in range(2):
        for tap in range(3):
            cols.append(vec(inp["ffn_conv"][l, tap]))
    for l in range(2):
        cols.append(vec(inp["ffn_conv_b"][l]))
    out = np.concatenate(cols, axis=1)
    assert out.shape == (128, PC_N)
    return np.ascontiguousarray(out)


def build_program(nseq=2, S=S_LEN, stop_after=None):
    NT = S // T
    nc = bass.Bass("TRN2", target_bir_lowering=False)

    def din(name, shape, dt=F32):
        return nc.dram_tensor(name, list(shape), dt, kind="ExternalInput").ap()

    def dscr(name, shape, dt):
        return nc.dram_tensor(name, list(shape), dt, kind="Internal").ap()

    x_fm = din("x_fm", [nseq, D, S])
    a_w_in = din("a_w_in", [D, 3 * D])
    a_w_out = din("a_w_out", [D, D])
    w_kv = din("w_kv", [D, 2 * D])
    b_w_q = din("b_w_q", [D, D])
    b_w_o = din("b_w_o", [D, D])
    w_up = [din("ffn_w_up%d" % l, [D, 2 * FF]) for l in range(2)]
    w_dn = [din("ffn_w_dn%d" % l, [FF, D]) for l in range(2)]
    rel_bias = din("rel_bias", [32, NH])
    prm_in = din("prm", [128, PC_N])
    onehot_in = din("onehot", [32, 1152])
    band_in = din("band", [16, 1152])
    jm_in = din("jm", [128, 128])
    out_fm = nc.dram_tensor("out_fm", [nseq, D, S], F32, kind="ExternalOutput").ap()

    s_win = dscr("s_win", [8, 128, KC * 384], BF16)
    s_wout = dscr("s_wout", [2, 128, KC * 512], BF16)
    s_wkv = dscr("s_wkv", [4, 128, KC * 512], BF16)
    s_wq = dscr("s_wq", [2, 128, KC * 512], BF16)
    s_wo = dscr("s_wo", [2, 128, KC * 512], BF16)
    s_wup = [dscr("s_wup%d" % l, [11, 128, KC * 512], BF16) for l in range(2)]
    s_wdn = [dscr("s_wdn%d" % l, [8, 128, FC * 128], BF16) for l in range(2)]
    s_h1 = dscr("s_h1", [nseq, D, S], F32)
    s_kT = dscr("s_kT", [nseq, D, S], BF16)
    s_qT = dscr("s_qT", [nseq, D, S], BF16)
    s_v = dscr("s_v", [nseq, S, D], BF16)
    s_at = dscr("s_at", [nseq, D, S], BF16)
    s_ev = dscr("s_ev", [16, 1152], F32)

    with ExitStack() as st:
        Sx = Sched(nc, st)
        op, dma = Sx.op, Sx.dma

        def sb(name, shape, dt):
            return st.enter_context(nc.sbuf_tensor(name, list(shape), dt))

        prm = sb("prm_sb", [128, PC_N], F32)
        jm = sb("jm_sb", [128, 128], F32)
        ones_bf = sb("ones_bf", [128, 128], BF16)
        mtab = sb("mtab", [128, 3 * NH * 2 * 128], BF16)
        hal_mix = sb("hal_mix", [128, 8, 2], F32)
        hal_ffn = sb("hal_ffn", [128, 2 * 44, 2], F32)
        NW = 5
        BFA = sb("bfa", [128, 4096 * 3 + 11264 + NW * 4096 + 4096 * 2], BF16)
        F32A = sb("f32a", [128, 4096 + 512 * 2 + 10 * 516], F32)
        psum = st.enter_context(nc.psum_tensor("psum", [128, 8, 512], F32))
        for b in range(8):
            Sx.excl.add(("ps", b))

        o = 0

        def carve(n):
            nonlocal o
            v = BFA[:, o:o + n]
            o += n
            return v

        xn = carve(4096).rearrange("p (a b) -> p a b", a=KC)
        yb = carve(4096).rearrange("p (a b) -> p a b", a=KC)
        sqb = carve(4096)
        ab = carve(11264).rearrange("p (a b) -> p a b", a=FC)
        wslots = [carve(4096) for _ in range(NW)]
        kts = carve(4096).rearrange("p (a b) -> p a b", a=KC)
        vs = carve(4096).rearrange("p (a b) -> p a b", a=4)
        h = F32A[:, 0:4096].rearrange("p (a b) -> p a b", a=KC)
        rstd_i = F32A[:, 4096:4608]
        rstd = F32A[:, 4608:5120]
        wk = [F32A[:, 5120 + i * 516: 5120 + (i + 1) * 516] for i in range(10)]

        qc = BFA[:, 0:2048]
        kcb = BFA[:, 2048:2048 + 4096]
        vc = BFA[:, 6144:6144 + 3 * 32 * 128].rearrange("p (l b f) -> p l b f", l=3, b=32)
        pT = [BFA[:, 18432 + i * 512: 18432 + (i + 1) * 512] for i in range(4)]
        ato = BFA[:, 20480:20480 + 2048]
        accn = F32A[:, 0:2048]
        accd = F32A[:, 2048:4096]
        rec = F32A[:, 5120:5120 + 2048]

        pctr = [0]

        def newbank():
            b = pctr[0] % 8
            pctr[0] += 1
            return b

        wctr = [0]

        def load_w(src, n, mkey):
            s = wctr[0] % NW
            wctr[0] += 1
            dma("sp", wslots[s][:, 0:n], src, reads=[mkey], writes=[("w", s)])
            return s

        dma("sp", prm[:], prm_in, writes=["prm"])
        dma("sp", jm[:], jm_in, writes=["jm"])
        op("dve", lambda e: e.memset(ones_bf[:], 1.0), writes=["ones"])
        op("dve", lambda e: e.memset(hal_mix[:], 0.0), writes=["hal_mix"])
        op("dve", lambda e: e.memset(hal_ffn[:], 0.0), writes=["hal_ffn"])

        def cast(dst, src, key):
            dma("pool", dst, src, commit_only=[key], semkey=key)

        for j in range(8):
            for t in range(3):
                cast(s_win[j].rearrange("p (kc t m) -> p kc t m", kc=KC, t=3)[:, :, t, :],
                     a_w_in.rearrange("(kc p) (t j m) -> j t p kc m", p=128, t=3, j=8)[j, t], "m_win")
        for b in range(2):
            cast(s_wout[b].rearrange("p (kc m) -> p kc m", kc=KC),
                 a_w_out.rearrange("(kc p) (b m) -> b p kc m", p=128, m=512)[b], "m_wout")

        def cast_ffn(l):
            for jb in range(11):
                for jj in range(2):
                    for half in range(2):
                        c0 = half * FF + (2 * jb + jj) * 128
                        cast(s_wup[l][jb].rearrange("p (kc s m) -> p kc s m", kc=KC, s=4)[:, :, jj * 2 + half, :],
                             w_up[l][:, c0:c0 + 128].rearrange("(kc p) m -> p kc m", p=128), "m_wup%d" % l)
            for j in range(8):
                cast(s_wdn[l][j].rearrange("p (kc m) -> p kc m", kc=FC),
                     w_dn[l].rearrange("(kc p) (j m) -> j p kc m", p=128, m=128)[j], "m_wdn%d" % l)

        cast_ffn(0)
        for b in range(4):
            cast(s_wkv[b].rearrange("p (kc m) -> p kc m", kc=KC),
                 w_kv.rearrange("(kc p) (b m) -> b p kc m", p=128, m=512)[b], "m_wkv")
        for b in range(2):
            cast(s_wq[b].rearrange("p (kc m) -> p kc m", kc=KC),
                 b_w_q.rearrange("(kc p) (b m) -> b p kc m", p=128, m=512)[b], "m_wq")
        for b in range(2):
            cast(s_wo[b].rearrange("p (kc m) -> p kc m", kc=KC),
                 b_w_o.rearrange("(kc p) (b m) -> b p kc m", p=128, m=512)[b], "m_wo")
        cast_ffn(1)

        def build_tables():
            rb = F32A[0:32, 0:16]
            oh = F32A[0:32, 16:16 + 1152]
            bd = F32A[0:16, 1200:1200 + 1152]
            ev = F32A[0:16, 2400:2400 + 1152]
            dma("sp", rb, rel_bias, writes=["tb_rb"])
            dma("sp", oh, onehot_in, writes=["tb_oh"])
            dma("sp", bd, band_in, writes=["tb_bd"])
            for di in range(3):
                b = newbank()
                op("pe", lambda e, b=b, di=di: e.matmul(psum[0:16, b, 0:384], lhsT=rb, rhs=oh[:, di * 384:(di + 1) * 384],
                                                       start=True, stop=True),
                   reads=["tb_rb", "tb_oh"], writes=[("ps", b)])
                op("act", lambda e, b=b, di=di: e.activation(out=ev[:, di * 384:(di + 1) * 384], in_=psum[0:16, b, 0:384], func=AF.Exp),
                   reads=[("ps", b)], writes=["tb_ev"])
            op("dve", lambda e: e.tensor_tensor(out=ev, in0=ev, in1=bd, op=ALU.mult), reads=["tb_ev", "tb_bd"], writes=["tb_ev"])
            dma("sp", s_ev, ev, reads=["tb_ev"], writes=["s_ev"])
            hk = F32A[:, 4096:4096 + 4096]
            mt = mtab[:].rearrange("p (d n) -> p d n", d=3)
            for di in range(3):
                src = bass.AP(tensor=s_ev.tensor, offset=di * 384, ap=[[1, 128], [1152, 16], [128, 2], [1, 128]])
                dma("sp", hk.rearrange("p (h k q) -> p h k q", h=16, k=2), src, reads=["s_ev"], writes=["tb_hk"])
                for c8 in range(8):
                    b = newbank()
                    op("pe", lambda e, b=b, c8=c8: e.matmul(psum[:, b, :], lhsT=jm[:], rhs=hk[:, c8 * 512:(c8 + 1) * 512],
                                                           start=True, stop=True),
                       reads=["jm", "tb_hk"], writes=[("ps", b)])
                    op("act", lambda e, b=b, c8=c8, di=di: e.copy(out=mt[:, di, c8 * 512:(c8 + 1) * 512], in_=psum[:, b, :]),
                       reads=[("ps", b)], writes=["mtab"])

        build_tables()
        Sx.barrier()

        def rmsnorm(pcol, out_bf=True, reuse=False):
            if not reuse:
                op("act", lambda e: e.activation(out=sqb, in_=h.rearrange("p a b -> p (a b)"), func=AF.Square),
                   reads=["h"], writes=["sqb"])
                b = newbank()
                for kc in range(KC):
                    op("pe", lambda e, kc=kc, b=b: e.matmul(psum[:, b, :], lhsT=ones_bf[:], rhs=sqb[:, kc * 512:(kc + 1) * 512],
                                                           start=(kc == 0), stop=(kc == KC - 1)),
                       reads=["ones", "sqb"], writes=[("ps", b)], signal=(kc == KC - 1))
                op("act", lambda e, b=b: e.activation(out=rstd_i, in_=psum[:, b, :], func=AF.Sqrt, scale=1.0 / D, bias=EPS),
                   reads=[("ps", b)], writes=["rstd_i"])
                op("dve", lambda e: e.reciprocal(out=rstd, in_=rstd_i), reads=["rstd_i"], writes=["rstd"])
            for kc in range(KC):
                if out_bf:
                    op("dve", lambda e, kc=kc: e.scalar_tensor_tensor(out=xn[:, kc, :], in0=h[:, kc, :], scalar=prm[:, pcol + kc:pcol + kc + 1],
                                                                       in1=rstd, op0=ALU.mult, op1=ALU.mult),
                       reads=["h", "rstd", "prm"], writes=["xn"])
                else:
                    op("dve", lambda e, kc=kc: e.scalar_tensor_tensor(out=h[:, kc, :], in0=h[:, kc, :], scalar=prm[:, pcol + kc:pcol + kc + 1],
                                                                       in1=rstd, op0=ALU.mult, op1=ALU.mult),
                       reads=["h", "rstd", "prm"], writes=["h"])

        def mm_group(b, wslot, woff, wstride, rhs_fn, nk, rkeys):
            for kc in range(nk):
                op("pe", lambda e, kc=kc: e.matmul(psum[:, b, :], lhsT=wslots[wslot][:, kc * wstride + woff: kc * wstride + woff + 128],
                                                  rhs=rhs_fn(kc), start=(kc == 0), stop=(kc == nk - 1)),
                   reads=[("w", wslot)] + rkeys, writes=[("ps", b)], signal=(kc == nk - 1))

        def conv_taps(acc, u, c0, c1, c2, bias=None):
            if bias is None:
                op("dve", lambda e: e.tensor_scalar(out=acc[:, 0:512], in0=u[:, 2:514], scalar1=prm[:, c2:c2 + 1], scalar2=None, op0=ALU.mult),
                   reads=[u_key(u), "prm"], writes=[u_key(acc)])
            else:
                op("dve", lambda e: e.tensor_scalar(out=acc[:, 0:512], in0=u[:, 2:514], scalar1=prm[:, c2:c2 + 1], scalar2=prm[:, bias:bias + 1],
                                                    op0=ALU.mult, op1=ALU.add),
                   reads=[u_key(u), "prm"], writes=[u_key(acc)])
            op("dve", lambda e: e.scalar_tensor_tensor(out=acc[:, 0:512], in0=u[:, 1:513], scalar=prm[:, c1:c1 + 1], in1=acc[:, 0:512],
                                                       op0=ALU.mult, op1=ALU.add),
               reads=[u_key(u), u_key(acc), "prm"], writes=[u_key(acc)])
            op("dve", lambda e: e.scalar_tensor_tensor(out=acc[:, 0:512], in0=u[:, 0:512], scalar=prm[:, c0:c0 + 1], in1=acc[:, 0:512],
                                                       op0=ALU.mult, op1=ALU.add),
               reads=[u_key(u), u_key(acc), "prm"], writes=[u_key(acc)])

        wk_ids = {}

        def u_key(ap):
            return ("wk", wk_ids[id(ap)])

        for i, w_ in enumerate(wk):
            wk_ids[id(w_)] = i

        def halo_io(u, hal_ap, hkey):
            op("pool", lambda e: e.tensor_copy(out=u[:, 0:2], in_=hal_ap), reads=[hkey], writes=[u_key(u)])
            op("pool", lambda e: e.tensor_copy(out=hal_ap, in_=u[:, 512:514]), reads=[u_key(u)], writes=[hkey])

        def residual_proj(s_w, mkey, rhs_buf, rkey, nk, per_block):
            if per_block == 4:
                for blk in range(2):
                    ws = load_w(s_w[blk], 4096, mkey)
                    for m in range(4):
                        j = blk * 4 + m
                        b = newbank()
                        mm_group(b, ws, m * 128, 512, lambda kc: rhs_buf[:, kc, :], nk, [rkey])
                        op("dve", lambda e, b=b, j=j: e.tensor_tensor(out=h[:, j, :], in0=psum[:, b, :], in1=h[:, j, :], op=ALU.add),
                           reads=[("ps", b), "h"], writes=["h"])
            else:
                for j in range(8):
                    ws = load_w(s_w[j], FC * 128, mkey)
                    b = newbank()
                    mm_group(b, ws, 0, 128, lambda kc: rhs_buf[:, kc, :], nk, [rkey])
                    op("dve", lambda e, b=b, j=j: e.tensor_tensor(out=h[:, j, :], in0=psum[:, b, :], in1=h[:, j, :], op=ALU.add),
                       reads=[("ps", b), "h"], writes=["h"])

        def ffn(l, pnorm):
            rmsnorm(pnorm)
            for jb in range(11):
                ws = load_w(s_wup[l][jb], 4096, "m_wup%d" % l)
                for jj in range(2):
                    j = 2 * jb + jj
                    par = j % 2
                    ug, uu, ag, au, sg = [wk[par * 5 + i] for i in range(5)]
                    bg, bu = newbank(), newbank()
                    mm_group(bg, ws, (jj * 2 + 0) * 128, 512, lambda kc: xn[:, kc, :], KC, ["xn"])
                    mm_group(bu, ws, (jj * 2 + 1) * 128, 512, lambda kc: xn[:, kc, :], KC, ["xn"])
                    for (bb, u, cidx) in ((bg, ug, j), (bu, uu, 22 + j)):
                        hal_ap = hal_ffn[:, l * 44 + cidx, :]
                        hkey = ("hf", l, cidx)
                        op("pool", lambda e, u=u, hal_ap=hal_ap: e.tensor_copy(out=u[:, 0:2], in_=hal_ap), reads=[hkey, "hal_ffn"], writes=[u_key(u)])
                        op("act", lambda e, u=u, bb=bb: e.copy(out=u[:, 2:514], in_=psum[:, bb, :]), reads=[("ps", bb)], writes=[u_key(u)])
                        op("pool", lambda e, u=u, hal_ap=hal_ap: e.tensor_copy(out=hal_ap, in_=u[:, 512:514]), reads=[u_key(u)], writes=[hkey])
                    base = PC_FCONV + l * 132
                    for (bb, u, acc, cidx) in ((bg, ug, ag, j), (bu, uu, au, 22 + j)):
                        c0, c1, c2, cb = base + cidx, base + 44 + cidx, base + 88 + cidx, PC_FBIAS + l * 44 + cidx
                        op("act", lambda e, acc=acc, bb=bb, c2=c2, cb=cb: e.activation(out=acc[:, 0:512], in_=psum[:, bb, :], func=AF.Identity,
                                                                                     scale=prm[:, c2:c2 + 1], bias=prm[:, cb:cb + 1]),
                           reads=[("ps", bb), "prm"], writes=[u_key(acc)])
                        op("dve", lambda e, acc=acc, u=u, c1=c1: e.scalar_tensor_tensor(out=acc[:, 0:512], in0=u[:, 1:513], scalar=prm[:, c1:c1 + 1],
                                                                                       in1=acc[:, 0:512], op0=ALU.mult, op1=ALU.add),
                           reads=[u_key(u), u_key(acc), "prm"], writes=[u_key(acc)])
                        op("dve", lambda e, acc=acc, u=u, c0=c0: e.scalar_tensor_tensor(out=acc[:, 0:512], in0=u[:, 0:512], scalar=prm[:, c0:c0 + 1],
                                                                                       in1=acc[:, 0:512], op0=ALU.mult, op1=ALU.add),
                           reads=[u_key(u), u_key(acc), "prm"], writes=[u_key(acc)])
                    op("act", lambda e, sg=sg, ag=ag: e.activation(out=sg[:, 0:512], in_=ag[:, 0:512], func=AF.Silu),
                       reads=[u_key(ag)], writes=[u_key(sg)])
                    op("pool", lambda e, sg=sg, au=au, j=j: e.tensor_tensor(out=ab[:, j, :], in0=sg[:, 0:512], in1=au[:, 0:512], op=ALU.mult),
                       reads=[u_key(sg), u_key(au)], writes=["ab"])
            residual_proj(s_wdn[l], "m_wdn%d" % l, ab, "ab", FC, 1)

        def tile_cols(ap3, t):
            return ap3[:, :, t * T:(t + 1) * T]

        def fm_view(d2):
            return d2.rearrange("(kc p) s -> p kc s", p=128)

        def phase_a(sq):
            op("dve", lambda e: e.memset(hal_mix[:], 0.0), writes=["hal_mix"] + [("hm", j) for j in range(8)])
            op("dve", lambda e: e.memset(hal_ffn[:], 0.0), writes=["hal_ffn"] + [("hf", 0, c) for c in range(44)])
            for t in range(NT):
                dma("sp", h, tile_cols(fm_view(x_fm[sq]), t), writes=["h"])
                rmsnorm(PC_ANORM)
                for j in range(8):
                    ws = load_w(s_win[j], KC * 384, "m_win")
                    bb_, bc_, bh_ = newbank(), newbank(), newbank()
                    mm_group(bb_, ws, 0, 384, lambda kc: xn[:, kc, :], KC, ["xn"])
                    mm_group(bc_, ws, 128, 384, lambda kc: xn[:, kc, :], KC, ["xn"])
                    mm_group(bh_, ws, 256, 384, lambda kc: xn[:, kc, :], KC, ["xn"])
                    par = j % 2
                    ch, hs, acc = wk[par * 5 + 0], wk[par * 5 + 1], wk[par * 5 + 2]
                    hal_ap = hal_mix[:, j, :]
                    hkey = ("hm", j)
                    op("act", lambda e, hs=hs, bh_=bh_: e.copy(out=hs[:, 0:512], in_=psum[:, bh_, :]), reads=[("ps", bh_)], writes=[u_key(hs)])
                    op("pool", lambda e, ch=ch, hal_ap=hal_ap: e.tensor_copy(out=ch[:, 0:2], in_=hal_ap), reads=[hkey, "hal_mix"], writes=[u_key(ch)])
                    op("dve", lambda e, ch=ch, hs=hs, bc_=bc_: e.tensor_tensor(out=ch[:, 2:514], in0=psum[:, bc_, :], in1=hs[:, 0:512], op=ALU.mult),
                       reads=[("ps", bc_), u_key(hs)], writes=[u_key(ch)])
                    op("pool", lambda e, ch=ch, hal_ap=hal_ap: e.tensor_copy(out=hal_ap, in_=ch[:, 512:514]), reads=[u_key(ch)], writes=[hkey])
                    conv_taps(acc, ch, PC_ACONV + j, PC_ACONV + 8 + j, PC_ACONV + 16 + j)
                    op("dve", lambda e, acc=acc, bb_=bb_, j=j: e.tensor_tensor(out=yb[:, j, :], in0=psum[:, bb_, :], in1=acc[:, 0:512], op=ALU.mult),
                       reads=[("ps", bb_), u_key(acc)], writes=["yb"])
                residual_proj(s_wout, "m_wout", yb, "yb", KC, 4)
                ffn(0, PC_FNORM0)
                rmsnorm(PC_KVNORM)
                for blk in range(2):
                    ws = load_w(s_wkv[blk], 4096, "m_wkv")
                    for m in range(4):
                        j = blk * 4 + m
                        b = newbank()
                        mm_group(b, ws, m * 128, 512, lambda kc: xn[:, kc, :], KC, ["xn"])
                        op("act", lambda e, b=b, j=j: e.copy(out=kts[:, j, :], in_=psum[:, b, :]), reads=[("ps", b)], writes=["kts"])
                dma("pool", tile_cols(fm_view(s_kT[sq]), t), kts, reads=["kts"], commit_only=[("s_kT", sq)], semkey="st_kts")
                for half in range(2):
                    ws = load_w(s_wkv[2 + half], 4096, "m_wkv")
                    for blk in range(4):
                        b = newbank()
                        for kc in range(KC):
                            op("pe", lambda e, kc=kc, b=b, blk=blk, ws=ws: e.matmul(psum[:, b, :], lhsT=xn[:, kc, blk * 128:(blk + 1) * 128],
                                                                                  rhs=wslots[ws][:, kc * 512:(kc + 1) * 512],
                                                                                  start=(kc == 0), stop=(kc == KC - 1)),
                               reads=[("w", ws), "xn"], writes=[("ps", b)], signal=(kc == KC - 1))
                        op("act", lambda e, b=b, blk=blk, half=half: e.copy(out=vs[:, blk, half * 512:(half + 1) * 512], in_=psum[:, b, :]),
                           reads=[("ps", b)], writes=["vs"])
                dma("pool", s_v[sq, t * T:(t + 1) * T, :].rearrange("(b p) f -> p b f", p=128), vs, reads=["vs"],
                    commit_only=[("s_v", sq)], semkey="st_vs")
                rmsnorm(PC_BNORM, reuse=True)
                for blk in range(2):
                    ws = load_w(s_wq[blk], 4096, "m_wq")
                    for m in range(4):
                        j = blk * 4 + m
                        b = newbank()
                        mm_group(b, ws, m * 128, 512, lambda kc: xn[:, kc, :], KC, ["xn"])
                        op("act", lambda e, b=b, j=j: e.copy(out=yb[:, j, :], in_=psum[:, b, :]), reads=[("ps", b)], writes=["yb"])
                dma("pool", tile_cols(fm_view(s_qT[sq]), t), yb, reads=["yb"], commit_only=[("s_qT", sq)], semkey="st_q")
                dma("pool", tile_cols(fm_view(s_h1[sq]), t), h, reads=["h"], commit_only=[("s_h1", sq)], semkey="st_h1")

        mt4 = mtab[:].rearrange("p (d h n) -> p d h n", d=3, h=NH)

        def phase_b2(sq):
            NHF = S // 2048
            for c in range(8):
                dma("sp", kcb[:, 0:S], s_kT[sq, c * 128:(c + 1) * 128, :], reads=[("s_kT", sq)], writes=["kcb"])
                vsrc = s_v[sq, :, c * 128:(c + 1) * 128]
                dma("sp", vc[:, 0, 0:S // 128, :], vsrc.rearrange("(b p) f -> p b f", p=128), reads=[("s_v", sq)], writes=["vc0"])
                for jj in range(S // 512):
                    dma("sp", vc[:, 1, jj * 4:(jj + 1) * 4, :], vsrc[jj * 512:(jj + 1) * 512, :].rearrange("(p r) f -> p r f", r=4),
                        reads=[("s_v", sq)], writes=[("vc1", jj)])
                for jj in range(NHF):
                    for rg in range(4):
                        dma("sp", vc[:, 2, jj * 16 + rg * 4: jj * 16 + rg * 4 + 4, :],
                            vsrc[jj * 2048:(jj + 1) * 2048, :].rearrange("(p r) f -> p r f", r=16)[:, rg * 4:(rg + 1) * 4, :],
                            reads=[("s_v", sq)], writes=[("vc2", jj, rg)])
                vkeys = ["vc0"] + [("vc1", jj) for jj in range(S // 512)] + [("vc2", jj, rg) for jj in range(NHF) for rg in range(4)]
                for hf in range(NHF):
                    dma("sp", qc, s_qT[sq, c * 128:(c + 1) * 128, hf * 2048:(hf + 1) * 2048], reads=[("s_qT", sq)], writes=["qc"])
                    groups = []
                    gi = 0
                    for di, d in enumerate(DILS):
                        for g in range(4):
                            gpar = (di * 4 + g) % 2
                            bn, bd_ = (0, 1) if gpar == 0 else (2, 3)
                            first = {0: True, 1: True}
                            for pr in range(2):
                                blks = []
                                for bi in range(2):
                                    u = g * 4 + pr * 2 + bi
                                    if d == 1:
                                        jb = hf * 16 + u
                                        qsl = slice(u * 128, (u + 1) * 128)
                                        kcur = slice(jb * 128, (jb + 1) * 128)
                                        jp = max(jb - 1, 0)
                                        kprev = slice(jp * 128, (jp + 1) * 128)
                                        vcur, vprev, has_prev = jb, jp, jb >= 1
                                    elif d == 4:
                                        lt, r = u // 4, u % 4
                                        j4 = hf * 4 + lt
                                        qsl = slice(lt * 512 + r, (lt + 1) * 512, 4)
                                        kcur = slice(j4 * 512 + r, (j4 + 1) * 512, 4)
                                        jp = max(j4 - 1, 0)
                                        kprev = slice(jp * 512 + r, (jp + 1) * 512, 4)
                                        vcur, vprev, has_prev = j4 * 4 + r, jp * 4 + r, j4 >= 1
                                    else:
                                        r = u
                                        qsl = slice(r, 2048, 16)
                                        kcur = slice(hf * 2048 + r, (hf + 1) * 2048, 16)
                                        jp = max(hf - 1, 0)
                                        kprev = slice(jp * 2048 + r, (jp + 1) * 2048, 16)
                                        vcur, vprev, has_prev = hf * 16 + r, jp * 16 + r, hf >= 1
                                    blks.append((qsl, kcur, kprev, vcur, vprev, has_prev))
                                spar = gi % 2
                                groups.append(dict(di=di, d=d, g=g, pr=pr, blks=blks, bs=[4 + spar * 2, 5 + spar * 2],
                                                   bn=bn, bd=bd_, first=first, spar=spar))
                                gi += 1

                    def emit_scores(G):
                        bs, di, spar = G["bs"], G["di"], G["spar"]
                        for bi, (qsl, kcur, kprev, vcur, vprev, has_prev) in enumerate(G["blks"]):
                            for kbp, ksl in enumerate((kcur, kprev)):
                                for hd in range(2):
                                    last = (bi == 1 and kbp == 1)
                                    op("pe", lambda e, hd=hd, bi=bi, kbp=kbp, ksl=ksl, qsl=qsl, bs=bs: e.matmul(
                                        psum[:, bs[hd], (bi * 2 + kbp) * 128:(bi * 2 + kbp + 1) * 128],
                                        lhsT=kcb[hd * 64:(hd + 1) * 64, ksl], rhs=qc[hd * 64:(hd + 1) * 64, qsl],
                                        start=True, stop=True),
                                       reads=["kcb", "qc"], writes=[("ps", bs[hd])], signal=last)
                        pts = []
                        for hd in range(2):
                            pt = pT[spar * 2 + hd]
                            pkey = ("pT", spar * 2 + hd)
                            pts.append((pt, pkey))
                            op("act", lambda e, pt=pt, hd=hd, bs=bs: e.activation(out=pt, in_=psum[:, bs[hd], :], func=AF.Exp, scale=0.125),
                               reads=[("ps", bs[hd])], writes=[pkey])
                            hh = c * 2 + hd
                            for bi in range(2):
                                op("dve", lambda e, pt=pt, bi=bi, hh=hh, di=di: e.tensor_tensor(
                                    out=pt[:, bi * 256:(bi + 1) * 256], in0=pt[:, bi * 256:(bi + 1) * 256],
                                    in1=mt4[:, di, hh, :], op=ALU.mult),
                                   reads=[pkey, "mtab"], writes=[pkey])
                        G["pts"] = pts

                    def emit_pv(G):
                        di, d, g, pr, bn, bd_, first, pts = G["di"], G["d"], G["g"], G["pr"], G["bn"], G["bd"], G["first"], G["pts"]
                        for bi, (qsl, kcur, kprev, vcur, vprev, has_prev) in enumerate(G["blks"]):
                            col = (pr * 2 + bi) * 128
                            for hd in range(2):
                                pt, pkey = pts[hd]
                                for kbp, vb in enumerate((vcur, vprev)):
                                    if kbp == 1 and not has_prev:
                                        continue
                                    rhs = pt[:, (bi * 2 + kbp) * 128:(bi * 2 + kbp + 1) * 128]
                                    stf = first[hd]
                                    first[hd] = False
                                    op("pe", lambda e, hd=hd, vb=vb, rhs=rhs, col=col, stf=stf, di=di, bn=bn: e.matmul(
                                        psum[hd * 64:(hd + 1) * 64, bn, col:col + 128],
                                        lhsT=vc[:, di, vb, hd * 64:(hd + 1) * 64], rhs=rhs, start=stf, stop=True,
                                        skip_group_check=True),
                                       reads=[pkey] + vkeys, writes=[("ps", bn)], signal=False)
                                    op("pe", lambda e, hd=hd, rhs=rhs, col=col, stf=stf, bd_=bd_: e.matmul(
                                        psum[hd * 64:(hd + 1) * 64, bd_, col:col + 128],
                                        lhsT=ones_bf[:, 0:64], rhs=rhs, start=stf, stop=True,
                                        skip_group_check=True),
                                       reads=[pkey, "ones"], writes=[("ps", bd_)], signal=True)
                        if pr != 1:
                            return
                        for (bk, acc, akey) in ((bn, accn, "accn"), (bd_, accd, "accd")):
                            if d == 1:
                                oap = acc[:, g * 512:(g + 1) * 512]
                                iap = psum[:, bk, :]
                            elif d == 4:
                                oap = acc[:, g * 512:(g + 1) * 512].rearrange("f (p r) -> f r p", r=4)
                                iap = psum[:, bk, :].rearrange("f (r p) -> f r p", r=4)
                            else:
                                oap = acc.rearrange("f (p r) -> f r p", r=16)[:, g * 4:(g + 1) * 4, :]
                                iap = psum[:, bk, :].rearrange("f (r p) -> f r p", r=4)
                            if d == 1:
                                op("act", lambda e, oap=oap, iap=iap: e.copy(out=oap, in_=iap), reads=[("ps", bk)], writes=[akey])
                            else:
                                op("dve", lambda e, oap=oap, iap=iap: e.tensor_tensor(out=oap, in0=iap, in1=oap, op=ALU.add),
                                   reads=[("ps", bk), akey], writes=[akey])

                    emit_scores(groups[0])
                    for i_, G in enumerate(groups):
                        if i_ + 1 < len(groups):
                            emit_scores(groups[i_ + 1])
                        emit_pv(G)
                    op("dve", lambda e: e.reciprocal(out=rec, in_=accd), reads=["accd"], writes=["rec"])
                    op("dve", lambda e: e.tensor_tensor(out=ato, in0=accn, in1=rec, op=ALU.mult), reads=["accn", "rec"], writes=["ato"])
                    dma("pool", s_at[sq, c * 128:(c + 1) * 128, hf * 2048:(hf + 1) * 2048], ato, reads=["ato"],
                        commit_only=[("s_at", sq)], semkey="st_ato")

        def phase_b3(sq):
            op("dve", lambda e: e.memset(hal_ffn[:], 0.0), writes=["hal_ffn"] + [("hf", 1, c) for c in range(44)])
            for t in range(NT):
                dma("sp", h, tile_cols(fm_view(s_h1[sq]), t), reads=[("s_h1", sq)], writes=["h"])
                dma("sp", yb, tile_cols(fm_view(s_at[sq]), t), reads=[("s_at", sq)], writes=["yb"])
                residual_proj(s_wo, "m_wo", yb, "yb", KC, 4)
                ffn(1, PC_FNORM1)
                rmsnorm(PC_FINAL, out_bf=False)
                dma("pool", tile_cols(fm_view(out_fm[sq]), t), h, reads=["h"], commit_only=["out"], semkey="st_out")

        for sq in range(nseq):
            if stop_after == "tables":
                break
            phase_a(sq)
            Sx.barrier()
            if stop_after == "a":
                break
            phase_b2(sq)
            Sx.barrier()
            if stop_after == "b2":
                break
            phase_b3(sq)
            Sx.barrier()
        Sx.barrier()
        Sx.emit()
    return nc


_NC_CACHE = {}


def kernel(**inputs):
    x = np.asarray(inputs["x"], np.float32)
    B = x.shape[0]
    ncores = 8
    nseq = B // ncores
    if "nc" not in _NC_CACHE:
        _NC_CACHE["nc"] = build_program(nseq=nseq)
    nc = _NC_CACHE["nc"]
    onehot, band, jm = _consts()
    prm = _pack_params(inputs)
    f32 = lambda a: np.ascontiguousarray(np.asarray(a, np.float32))
    shared = {
        "a_w_in": f32(inputs["a_w_in"][0]),
        "a_w_out": f32(inputs["a_w_out"][0]),
        "w_kv": f32(inputs["w_kv"]),
        "b_w_q": f32(inputs["b_w_q"][0]),
        "b_w_o": f32(inputs["b_w_o"][0]),
        "ffn_w_up0": f32(inputs["ffn_w_up"][0]),
        "ffn_w_up1": f32(inputs["ffn_w_up"][1]),
        "ffn_w_dn0": f32(inputs["ffn_w_down"][0]),
        "ffn_w_dn1": f32(inputs["ffn_w_down"][1]),
        "rel_bias": f32(inputs["rel_bias"]),
        "prm": prm, "onehot": onehot, "band": band, "jm": jm,
    }
    in_maps = []
    for c in range(ncores):
        m = dict(shared)
        m["x_fm"] = np.ascontiguousarray(x[c * nseq:(c + 1) * nseq].transpose(0, 2, 1))
        in_maps.append(m)
    res = run_bass_kernel_spmd(nc, in_maps, core_ids=list(range(ncores)))
    outs = [np.asarray(r["out_fm"]).transpose(0, 2, 1) for r in res.results]
    return np.ascontiguousarray(np.concatenate(outs, axis=0).astype(np.float32))
```

```python
import math
import numpy as np
import concourse.bass as bass
import concourse.mybir as mybir
from concourse.bass_utils import run_bass_kernel_spmd
from contextlib import ExitStack

F32 = mybir.dt.float32
BF16 = mybir.dt.bfloat16
ALU = mybir.AluOpType
AF = mybir.ActivationFunctionType

D = 1024
FF = 2816
S_LEN = 4096
T = 512
KC = 8
FC = 22
NH = 16
EPS = 1e-6
DILS = (1, 4, 16)
ENGS = ("pe", "act", "dve", "pool", "sp")


class Sched:
    def __init__(self, nc, stack):
        self.nc = nc
        self.stack = stack
        self.ops = {e: [] for e in ENGS}
        self.esem = {e: stack.enter_context(nc.semaphore("prog_" + e)) for e in ENGS}
        self.ecnt = {e: 0 for e in ENGS}
        self.waited = {e: {} for e in ENGS}
        self.res = {}
        self.dsem = {}
        self.dcnt = {}
        self.excl = set()

    def _need(self, eng, toks):
        w = self.waited[eng]
        best = {}
        for t in toks:
            if t is None:
                continue
            key, sem, val, _ = t
            if w.get(key, 0) >= val:
                continue
            if key not in best or best[key][2] < val:
                best[key] = t
        for t in best.values():
            w[t[0]] = t[2]
        return list(best.values())

    def _deps(self, eng, reads, writes):
        toks = []
        for r in reads:
            st = self.res.get(r)
            if st is None:
                continue
            toks.append(st["w"])
            if r in self.excl:
                toks.extend(t for t in st["r"] if t[3] != eng)
        for wkey in writes:
            st = self.res.get(wkey)
            if st is None:
                continue
            tw = st["w"]
            if tw is not None and tw[3] != eng:
                toks.append(tw)
            for tr in st["r"]:
                if tr[3] != eng:
                    toks.append(tr)
        return toks

    def _commit(self, tok, reads, writes):
        for r in reads:
            st = self.res.setdefault(r, {"w": None, "r": []})
            st["r"] = [t for t in st["r"] if t[0] != tok[0]] + [tok]
        for wkey in writes:
            self.res[wkey] = {"w": tok, "r": []}

    def op(self, eng, fn, reads=(), writes=(), signal=True):
        toks = self._need(eng, self._deps(eng, reads, writes))
        n = self.ecnt[eng] + 1
        if signal:
            self.ecnt[eng] = n
        tok = ("E" + eng, self.esem[eng], n, eng)
        self.ops[eng].append((toks, fn, self.esem[eng] if signal else None, 1))
        self._commit(tok, reads, writes)
        return tok

    def dma(self, eng, out, in_, reads=(), writes=(), semkey=None, commit_only=()):
        if semkey is None:
            semkey = (writes[0] if writes else (commit_only[0] if commit_only else reads[0]))
        if semkey not in self.dsem:
            self.dsem[semkey] = self.stack.enter_context(
                self.nc.semaphore("d%d" % len(self.dsem)))
            self.dcnt[semkey] = 0
        toks = self._need(eng, self._deps(eng, reads, writes))
        self.dcnt[semkey] += 16
        sem = self.dsem[semkey]
        tok = ("D" + str(semkey), sem, self.dcnt[semkey], None)

        def fn(e, out=out, in_=in_):
            return e.dma_start(out=out, in_=in_)

        self.ops[eng].append((toks, fn, sem, 16))
        self._commit(tok, reads, list(writes) + list(commit_only))
        return tok

    def barrier(self):
        toks = [("E" + e, self.esem[e], self.ecnt[e], e) for e in ENGS if self.ecnt[e] > 0]
        toks += [("D" + str(k), self.dsem[k], self.dcnt[k], None) for k in self.dsem if self.dcnt[k] > 0]
        for e in ENGS:
            need = self._need(e, [t for t in toks if t[3] != e])
            if need:
                self.ops[e].append((need, None, None, 0))

    def emit(self):
        nc = self.nc
        with nc.Block() as block:
            def run(engobj, name):
                for toks, fn, sem, inc in self.ops[name]:
                    for (_, s, v, _) in toks:
                        engobj.wait_ge(s, v)
                    if fn is None:
                        continue
                    ins = fn(engobj)
                    if sem is not None:
                        ins.then_inc(sem, inc)

            @block.tensor
            def _(e):
                run(e, "pe")

            @block.scalar
            def _(e):
                run(e, "act")

            @block.vector
            def _(e):
                run(e, "dve")

            @block.gpsimd
            def _(e):
                run(e, "pool")

            @block.sync
            def _(e):
                run(e, "sp")


def _t5_bucket(n):
    if n < 16:
        return n
    v = np.log(np.float32(n) / np.float32(16)) / np.float32(math.log(128.0)) * np.float32(16)
    return min(31, 16 + int(np.float32(v)))


def _consts():
    onehot = np.zeros((32, 3 * 384), np.float32)
    band = np.zeros((16, 3 * 384), np.float32)
    for di, d in enumerate(DILS):
        for i in range(384):
            delta = i - 127
            if 0 <= delta <= 128:
                onehot[_t5_bucket(delta * d), di * 384 + i] = 1.0
                band[:, di * 384 + i] = 1.0
    jm = np.zeros((128, 128), np.float32)
    for p in range(128):
        jm[p, 127 - p] = 1.0
    return onehot, band, jm


PC_ANORM, PC_FNORM0, PC_KVNORM, PC_BNORM, PC_FNORM1, PC_FINAL = 0, 8, 16, 24, 32, 40
PC_ACONV = 48
PC_FCONV = 72
PC_FBIAS = 72 + 264
PC_N = PC_FBIAS + 88


def _pack_params(inp):
    cols = []

    def vec(v):
        return np.ascontiguousarray(np.asarray(v, np.float32).reshape(-1, 128).T)

    cols.append(vec(inp["a_norm"][0]))
    cols.append(vec(inp["ffn_norm"][0]))
    cols.append(vec(inp["kv_norm"]))
    cols.append(vec(inp["b_norm"][0]))
    cols.append(vec(inp["ffn_norm"][1]))
    cols.append(vec(inp["final_norm"]))
    for tap in range(3):
        cols.append(vec(inp["a_conv"][0, tap]))
    for l in range(2):
        for tap in range(3):
            cols.append(vec(inp["ffn_conv"][l, tap]))
    for l in range(2):
        cols.append(vec(inp["ffn_conv_b"][l]))
    out = np.concatenate(cols, axis=1)
    assert out.shape == (128, PC_N)
    return np.ascontiguousarray(out)


def build_program(nseq=2, S=S_LEN, stop_after=None):
    NT = S // T
    nc = bass.Bass("TRN2", target_bir_lowering=False)

    def din(name, shape, dt=F32):
        return nc.dram_tensor(name, list(shape), dt, kind="ExternalInput").ap()

    def dscr(name, shape, dt):
        return nc.dram_tensor(name, list(shape), dt, kind="Internal").ap()

    x_fm = din("x_fm", [nseq, D, S])
    a_w_in = din("a_w_in", [D, 3 * D])
    a_w_out = din("a_w_out", [D, D])
    w_kv = din("w_kv", [D, 2 * D])
    b_w_q = din("b_w_q", [D, D])
    b_w_o = din("b_w_o", [D, D])
    w_up = [din("ffn_w_up%d" % l, [D, 2 * FF]) for l in range(2)]
    w_dn = [din("ffn_w_dn%d" % l, [FF, D]) for l in range(2)]
    rel_bias = din("rel_bias", [32, NH])
    prm_in = din("prm", [128, PC_N])
    onehot_in = din("onehot", [32, 1152])
    band_in = din("band", [16, 1152])
    jm_in = din("jm", [128, 128])
    out_fm = nc.dram_tensor("out_fm", [nseq, D, S], F32, kind="ExternalOutput").ap()

    s_win = dscr("s_win", [8, 128, KC * 384], BF16)
    s_wout = dscr("s_wout", [2, 128, KC * 512], BF16)
    s_wkv = dscr("s_wkv", [4, 128, KC * 512], BF16)
    s_wq = dscr("s_wq", [2, 128, KC * 512], BF16)
    s_wo = dscr("s_wo", [2, 128, KC * 512], BF16)
    s_wup = [dscr("s_wup%d" % l, [11, 128, KC * 512], BF16) for l in range(2)]
    s_wdn = [dscr("s_wdn%d" % l, [8, 128, FC * 128], BF16) for l in range(2)]
    s_h1 = dscr("s_h1", [nseq, D, S], F32)
    s_kT = dscr("s_kT", [nseq, D, S], BF16)
    s_qT = dscr("s_qT", [nseq, D, S], BF16)
    s_v = dscr("s_v", [nseq, S, D], BF16)
    s_at = dscr("s_at", [nseq, D, S], BF16)
    s_ev = dscr("s_ev", [16, 1152], F32)

    with ExitStack() as st:
        Sx = Sched(nc, st)
        op, dma = Sx.op, Sx.dma

        def sb(name, shape, dt):
            return st.enter_context(nc.sbuf_tensor(name, list(shape), dt))

        prm = sb("prm_sb", [128, PC_N], F32)
        jm = sb("jm_sb", [128, 128], F32)
        ones_bf = sb("ones_bf", [128, 128], BF16)
        mtab = sb("mtab", [128, 3 * NH * 2 * 128], BF16)
        hal_mix = sb("hal_mix", [128, 8, 2], F32)
        hal_ffn = sb("hal_ffn", [128, 2 * 44, 2], F32)
        NW = 5
        BFA = sb("bfa", [128, 4096 * 3 + 11264 + NW * 4096 + 4096 * 2], BF16)
        F32A = sb("f32a", [128, 4096 + 512 * 2 + 10 * 516], F32)
        psum = st.enter_context(nc.psum_tensor("psum", [128, 8, 512], F32))
        for b in range(8):
            Sx.excl.add(("ps", b))

        o = 0

        def carve(n):
            nonlocal o
            v = BFA[:, o:o + n]
            o += n
            return v

        xn = carve(4096).rearrange("p (a b) -> p a b", a=KC)
        yb = carve(4096).rearrange("p (a b) -> p a b", a=KC)
        sqb = carve(4096)
        ab = carve(11264).rearrange("p (a b) -> p a b", a=FC)
        wslots = [carve(4096) for _ in range(NW)]
        kts = carve(4096).rearrange("p (a b) -> p a b", a=KC)
        vs = carve(4096).rearrange("p (a b) -> p a b", a=4)
        h = F32A[:, 0:4096].rearrange("p (a b) -> p a b", a=KC)
        rstd_i = F32A[:, 4096:4608]
        rstd = F32A[:, 4608:5120]
        wk = [F32A[:, 5120 + i * 516: 5120 + (i + 1) * 516] for i in range(10)]

        qc = BFA[:, 0:2048]
        kcb = BFA[:, 2048:2048 + 4096]
        vc = BFA[:, 6144:6144 + 3 * 32 * 128].rearrange("p (l b f) -> p l b f", l=3, b=32)
        pT = [BFA[:, 18432 + i * 512: 18432 + (i + 1) * 512] for i in range(4)]
        ato = BFA[:, 20480:20480 + 2048]
        accn = F32A[:, 0:2048]
        accd = F32A[:, 2048:4096]
        rec = F32A[:, 5120:5120 + 2048]

        pctr = [0]

        def newbank():
            b = pctr[0] % 8
            pctr[0] += 1
            return b

        wctr = [0]

        def load_w(src, n, mkey):
            s = wctr[0] % NW
            wctr[0] += 1
            dma("sp", wslots[s][:, 0:n], src, reads=[mkey], writes=[("w", s)])
            return s

        dma("sp", prm[:], prm_in, writes=["prm"])
        dma("sp", jm[:], jm_in, writes=["jm"])
        op("dve", lambda e: e.memset(ones_bf[:], 1.0), writes=["ones"])
        op("dve", lambda e: e.memset(hal_mix[:], 0.0), writes=["hal_mix"])
        op("dve", lambda e: e.memset(hal_ffn[:], 0.0), writes=["hal_ffn"])

        def build_tables():
            rb = F32A[0:32, 0:16]
            oh = F32A[0:32, 16:16 + 1152]
            bd = F32A[0:16, 1200:1200 + 1152]
            ev = F32A[0:16, 2400:2400 + 1152]
            dma("sp", rb, rel_bias, writes=["tb_rb"])
            dma("sp", oh, onehot_in, writes=["tb_oh"])
            dma("sp", bd, band_in, writes=["tb_bd"])
            for di in range(3):
                b = newbank()
                op("pe", lambda e, b=b, di=di: e.matmul(psum[0:16, b, 0:384], lhsT=rb, rhs=oh[:, di * 384:(di + 1) * 384],
                                                       start=True, stop=True),
                   reads=["tb_rb", "tb_oh"], writes=[("ps", b)])
                op("act", lambda e, b=b, di=di: e.activation(out=ev[:, di * 384:(di + 1) * 384], in_=psum[0:16, b, 0:384], func=AF.Exp),
                   reads=[("ps", b)], writes=["tb_ev"])
            op("dve", lambda e: e.tensor_tensor(out=ev, in0=ev, in1=bd, op=ALU.mult), reads=["tb_ev", "tb_bd"], writes=["tb_ev"])
            dma("sp", s_ev, ev, reads=["tb_ev"], writes=["s_ev"])
            hk = F32A[:, 4096:4096 + 4096]
            mt = mtab[:].rearrange("p (d n) -> p d n", d=3)
            for di in range(3):
                src = bass.AP(tensor=s_ev.tensor, offset=di * 384, ap=[[1, 128], [1152, 16], [128, 2], [1, 128]])
                dma("sp", hk.rearrange("p (h k q) -> p h k q", h=16, k=2), src, reads=["s_ev"], writes=["tb_hk"])
                for c8 in range(8):
                    b = newbank()
                    op("pe", lambda e, b=b, c8=c8: e.matmul(psum[:, b, :], lhsT=jm[:], rhs=hk[:, c8 * 512:(c8 + 1) * 512],
                                                           start=True, stop=True),
                       reads=["jm", "tb_hk"], writes=[("ps", b)])
                    op("act", lambda e, b=b, c8=c8, di=di: e.copy(out=mt[:, di, c8 * 512:(c8 + 1) * 512], in_=psum[:, b, :]),
                       reads=[("ps", b)], writes=["mtab"])

        build_tables()
        Sx.barrier()

        def cast(dst, src, key):
            dma("pool", dst, src, commit_only=[key], semkey=key)

        for j in range(8):
            for t in range(3):
                cast(s_win[j].rearrange("p (kc t m) -> p kc t m", kc=KC, t=3)[:, :, t, :],
                     a_w_in.rearrange("(kc p) (t j m) -> j t p kc m", p=128, t=3, j=8)[j, t], "m_win")
        for b in range(2):
            cast(s_wout[b].rearrange("p (kc m) -> p kc m", kc=KC),
                 a_w_out.rearrange("(kc p) (b m) -> b p kc m", p=128, m=512)[b], "m_wout")

        def cast_ffn(l):
            for jb in range(11):
                for jj in range(2):
                    for half in range(2):
                        c0 = half * FF + (2 * jb + jj) * 128
                        cast(s_wup[l][jb].rearrange("p (kc s m) -> p kc s m", kc=KC, s=4)[:, :, jj * 2 + half, :],
                             w_up[l][:, c0:c0 + 128].rearrange("(kc p) m -> p kc m", p=128), "m_wup%d" % l)
            for j in range(8):
                cast(s_wdn[l][j].rearrange("p (kc m) -> p kc m", kc=FC),
                     w_dn[l].rearrange("(kc p) (j m) -> j p kc m", p=128, m=128)[j], "m_wdn%d" % l)

        cast_ffn(0)
        for b in range(4):
            cast(s_wkv[b].rearrange("p (kc m) -> p kc m", kc=KC),
                 w_kv.rearrange("(kc p) (b m) -> b p kc m", p=128, m=512)[b], "m_wkv")
        for b in range(2):
            cast(s_wq[b].rearrange("p (kc m) -> p kc m", kc=KC),
                 b_w_q.rearrange("(kc p) (b m) -> b p kc m", p=128, m=512)[b], "m_wq")
        for b in range(2):
            cast(s_wo[b].rearrange("p (kc m) -> p kc m", kc=KC),
                 b_w_o.rearrange("(kc p) (b m) -> b p kc m", p=128, m=512)[b], "m_wo")
        cast_ffn(1)


        def rmsnorm(pcol, out_bf=True, reuse=False):
            if not reuse:
                op("act", lambda e: e.activation(out=sqb, in_=h.rearrange("p a b -> p (a b)"), func=AF.Square),
                   reads=["h"], writes=["sqb"])
                b = newbank()
                for kc in range(KC):
                    op("pe", lambda e, kc=kc, b=b: e.matmul(psum[:, b, :], lhsT=ones_bf[:], rhs=sqb[:, kc * 512:(kc + 1) * 512],
                                                           start=(kc == 0), stop=(kc == KC - 1)),
                       reads=["ones", "sqb"], writes=[("ps", b)], signal=(kc == KC - 1))
                op("act", lambda e, b=b: e.activation(out=rstd_i, in_=psum[:, b, :], func=AF.Sqrt, scale=1.0 / D, bias=EPS),
                   reads=[("ps", b)], writes=["rstd_i"])
                op("dve", lambda e: e.reciprocal(out=rstd, in_=rstd_i), reads=["rstd_i"], writes=["rstd"])
            for kc in range(KC):
                if out_bf:
                    op("dve", lambda e, kc=kc: e.scalar_tensor_tensor(out=xn[:, kc, :], in0=h[:, kc, :], scalar=prm[:, pcol + kc:pcol + kc + 1],
                                                                       in1=rstd, op0=ALU.mult, op1=ALU.mult),
                       reads=["h", "rstd", "prm"], writes=["xn"])
                else:
                    op("dve", lambda e, kc=kc: e.scalar_tensor_tensor(out=h[:, kc, :], in0=h[:, kc, :], scalar=prm[:, pcol + kc:pcol + kc + 1],
                                                                       in1=rstd, op0=ALU.mult, op1=ALU.mult),
                       reads=["h", "rstd", "prm"], writes=["h"])

        def mm_group(b, wslot, woff, wstride, rhs_fn, nk, rkeys):
            for kc in range(nk):
                op("pe", lambda e, kc=kc: e.matmul(psum[:, b, :], lhsT=wslots[wslot][:, kc * wstride + woff: kc * wstride + woff + 128],
                                                  rhs=rhs_fn(kc), start=(kc == 0), stop=(kc == nk - 1)),
                   reads=[("w", wslot)] + rkeys, writes=[("ps", b)], signal=(kc == nk - 1))

        def conv_taps(acc, u, c0, c1, c2, bias=None):
            if bias is None:
                op("dve", lambda e: e.tensor_scalar(out=acc[:, 0:512], in0=u[:, 2:514], scalar1=prm[:, c2:c2 + 1], scalar2=None, op0=ALU.mult),
                   reads=[u_key(u), "prm"], writes=[u_key(acc)])
            else:
                op("dve", lambda e: e.tensor_scalar(out=acc[:, 0:512], in0=u[:, 2:514], scalar1=prm[:, c2:c2 + 1], scalar2=prm[:, bias:bias + 1],
                                                    op0=ALU.mult, op1=ALU.add),
                   reads=[u_key(u), "prm"], writes=[u_key(acc)])
            op("dve", lambda e: e.scalar_tensor_tensor(out=acc[:, 0:512], in0=u[:, 1:513], scalar=prm[:, c1:c1 + 1], in1=acc[:, 0:512],
                                                       op0=ALU.mult, op1=ALU.add),
               reads=[u_key(u), u_key(acc), "prm"], writes=[u_key(acc)])
            op("dve", lambda e: e.scalar_tensor_tensor(out=acc[:, 0:512], in0=u[:, 0:512], scalar=prm[:, c0:c0 + 1], in1=acc[:, 0:512],
                                                       op0=ALU.mult, op1=ALU.add),
               reads=[u_key(u), u_key(acc), "prm"], writes=[u_key(acc)])

        wk_ids = {}

        def u_key(ap):
            return ("wk", wk_ids[id(ap)])

        for i, w_ in enumerate(wk):
            wk_ids[id(w_)] = i

        def halo_io(u, hal_ap, hkey):
            op("pool", lambda e: e.tensor_copy(out=u[:, 0:2], in_=hal_ap), reads=[hkey], writes=[u_key(u)])
            op("pool", lambda e: e.tensor_copy(out=hal_ap, in_=u[:, 512:514]), reads=[u_key(u)], writes=[hkey])

        def residual_proj(s_w, mkey, rhs_buf, rkey, nk, per_block):
            if per_block == 4:
                for blk in range(2):
                    ws = load_w(s_w[blk], 4096, mkey)
                    for m in range(4):
                        j = blk * 4 + m
                        b = newbank()
                        mm_group(b, ws, m * 128, 512, lambda kc: rhs_buf[:, kc, :], nk, [rkey])
                        op("dve", lambda e, b=b, j=j: e.tensor_tensor(out=h[:, j, :], in0=psum[:, b, :], in1=h[:, j, :], op=ALU.add),
                           reads=[("ps", b), "h"], writes=["h"])
            else:
                for j in range(8):
                    ws = load_w(s_w[j], FC * 128, mkey)
                    b = newbank()
                    mm_group(b, ws, 0, 128, lambda kc: rhs_buf[:, kc, :], nk, [rkey])
                    op("dve", lambda e, b=b, j=j: e.tensor_tensor(out=h[:, j, :], in0=psum[:, b, :], in1=h[:, j, :], op=ALU.add),
                       reads=[("ps", b), "h"], writes=["h"])

        def ffn(l, pnorm):
            rmsnorm(pnorm)
            pend = None
            for jb in range(11):
                ws = load_w(s_wup[l][jb], 4096, "m_wup%d" % l)
                for jj in range(2):
                    j = 2 * jb + jj
                    par = j % 2
                    ug, uu, ag, au, sg = [wk[par * 5 + i] for i in range(5)]
                    bg, bu = newbank(), newbank()
                    mm_group(bg, ws, (jj * 2 + 0) * 128, 512, lambda kc: xn[:, kc, :], KC, ["xn"])
                    mm_group(bu, ws, (jj * 2 + 1) * 128, 512, lambda kc: xn[:, kc, :], KC, ["xn"])
                    for (bb, u, cidx) in ((bg, ug, j), (bu, uu, 22 + j)):
                        hal_ap = hal_ffn[:, l * 44 + cidx, :]
                        hkey = ("hf", l, cidx)
                        op("pool", lambda e, u=u, hal_ap=hal_ap: e.tensor_copy(out=u[:, 0:2], in_=hal_ap), reads=[hkey, "hal_ffn"], writes=[u_key(u)])
                        op("act", lambda e, u=u, bb=bb: e.copy(out=u[:, 2:514], in_=psum[:, bb, :]), reads=[("ps", bb)], writes=[u_key(u)])
                        op("pool", lambda e, u=u, hal_ap=hal_ap: e.tensor_copy(out=hal_ap, in_=u[:, 512:514]), reads=[u_key(u)], writes=[hkey])
                    base = PC_FCONV + l * 132
                    for (bb, u, acc, cidx) in ((bg, ug, ag, j), (bu, uu, au, 22 + j)):
                        c0, c1, c2, cb = base + cidx, base + 44 + cidx, base + 88 + cidx, PC_FBIAS + l * 44 + cidx
                        op("act", lambda e, acc=acc, bb=bb, c2=c2, cb=cb: e.activation(out=acc[:, 0:512], in_=psum[:, bb, :], func=AF.Identity,
                                                                                     scale=prm[:, c2:c2 + 1], bias=prm[:, cb:cb + 1]),
                           reads=[("ps", bb), "prm"], writes=[u_key(acc)])
                        op("dve", lambda e, acc=acc, u=u, c1=c1: e.scalar_tensor_tensor(out=acc[:, 0:512], in0=u[:, 1:513], scalar=prm[:, c1:c1 + 1],
                                                                                       in1=acc[:, 0:512], op0=ALU.mult, op1=ALU.add),
                           reads=[u_key(u), u_key(acc), "prm"], writes=[u_key(acc)])
                        op("dve", lambda e, acc=acc, u=u, c0=c0: e.scalar_tensor_tensor(out=acc[:, 0:512], in0=u[:, 0:512], scalar=prm[:, c0:c0 + 1],
                                                                                       in1=acc[:, 0:512], op0=ALU.mult, op1=ALU.add),
                           reads=[u_key(u), u_key(acc), "prm"], writes=[u_key(acc)])
                    if pend is not None:
                        ffn_tail(*pend)
                    pend = (sg, ag, au, j)
            ffn_tail(*pend)
            residual_proj(s_wdn[l], "m_wdn%d" % l, ab, "ab", FC, 1)

        def ffn_tail(sg, ag, au, j):
            op("act", lambda e, sg=sg, ag=ag: e.activation(out=sg[:, 0:512], in_=ag[:, 0:512], func=AF.Silu),
               reads=[u_key(ag)], writes=[u_key(sg)])
            op("pool", lambda e, sg=sg, au=au, j=j: e.tensor_tensor(out=ab[:, j, :], in0=sg[:, 0:512], in1=au[:, 0:512], op=ALU.mult),
               reads=[u_key(sg), u_key(au)], writes=["ab"])

        def tile_cols(ap3, t):
            return ap3[:, :, t * T:(t + 1) * T]

        def fm_view(d2):
            return d2.rearrange("(kc p) s -> p kc s", p=128)

        def phase_a(sq):
            op("dve", lambda e: e.memset(hal_mix[:], 0.0), writes=["hal_mix"] + [("hm", j) for j in range(8)])
            op("dve", lambda e: e.memset(hal_ffn[:], 0.0), writes=["hal_ffn"] + [("hf", 0, c) for c in range(44)])
            for t in range(NT):
                dma("sp", h, tile_cols(fm_view(x_fm[sq]), t), writes=["h"])
                rmsnorm(PC_ANORM)
                for j in range(8):
                    ws = load_w(s_win[j], KC * 384, "m_win")
                    bb_, bc_, bh_ = newbank(), newbank(), newbank()
                    mm_group(bb_, ws, 0, 384, lambda kc: xn[:, kc, :], KC, ["xn"])
                    mm_group(bc_, ws, 128, 384, lambda kc: xn[:, kc, :], KC, ["xn"])
                    mm_group(bh_, ws, 256, 384, lambda kc: xn[:, kc, :], KC, ["xn"])
                    par = j % 2
                    ch, hs, acc = wk[par * 5 + 0], wk[par * 5 + 1], wk[par * 5 + 2]
                    hal_ap = hal_mix[:, j, :]
                    hkey = ("hm", j)
                    op("act", lambda e, hs=hs, bh_=bh_: e.copy(out=hs[:, 0:512], in_=psum[:, bh_, :]), reads=[("ps", bh_)], writes=[u_key(hs)])
                    op("pool", lambda e, ch=ch, hal_ap=hal_ap: e.tensor_copy(out=ch[:, 0:2], in_=hal_ap), reads=[hkey, "hal_mix"], writes=[u_key(ch)])
                    op("dve", lambda e, ch=ch, hs=hs, bc_=bc_: e.tensor_tensor(out=ch[:, 2:514], in0=psum[:, bc_, :], in1=hs[:, 0:512], op=ALU.mult),
                       reads=[("ps", bc_), u_key(hs)], writes=[u_key(ch)])
                    op("pool", lambda e, ch=ch, hal_ap=hal_ap: e.tensor_copy(out=hal_ap, in_=ch[:, 512:514]), reads=[u_key(ch)], writes=[hkey])
                    conv_taps(acc, ch, PC_ACONV + j, PC_ACONV + 8 + j, PC_ACONV + 16 + j)
                    op("dve", lambda e, acc=acc, bb_=bb_, j=j: e.tensor_tensor(out=yb[:, j, :], in0=psum[:, bb_, :], in1=acc[:, 0:512], op=ALU.mult),
                       reads=[("ps", bb_), u_key(acc)], writes=["yb"])
                residual_proj(s_wout, "m_wout", yb, "yb", KC, 4)
                ffn(0, PC_FNORM0)
                rmsnorm(PC_KVNORM)
                for blk in range(2):
                    ws = load_w(s_wkv[blk], 4096, "m_wkv")
                    for m in range(4):
                        j = blk * 4 + m
                        b = newbank()
                        mm_group(b, ws, m * 128, 512, lambda kc: xn[:, kc, :], KC, ["xn"])
                        op("act", lambda e, b=b, j=j: e.copy(out=kts[:, j, :], in_=psum[:, b, :]), reads=[("ps", b)], writes=["kts"])
                dma("pool", tile_cols(fm_view(s_kT[sq]), t), kts, reads=["kts"], commit_only=[("s_kT", sq)], semkey="st_kts")
                for half in range(2):
                    ws = load_w(s_wkv[2 + half], 4096, "m_wkv")
                    for blk in range(4):
                        b = newbank()
                        for kc in range(KC):
                            op("pe", lambda e, kc=kc, b=b, blk=blk, ws=ws: e.matmul(psum[:, b, :], lhsT=xn[:, kc, blk * 128:(blk + 1) * 128],
                                                                                  rhs=wslots[ws][:, kc * 512:(kc + 1) * 512],
                                                                                  start=(kc == 0), stop=(kc == KC - 1)),
                               reads=[("w", ws), "xn"], writes=[("ps", b)], signal=(kc == KC - 1))
                        op("act", lambda e, b=b, blk=blk, half=half: e.copy(out=vs[:, blk, half * 512:(half + 1) * 512], in_=psum[:, b, :]),
                           reads=[("ps", b)], writes=["vs"])
                dma("pool", s_v[sq, t * T:(t + 1) * T, :].rearrange("(b p) f -> p b f", p=128), vs, reads=["vs"],
                    commit_only=[("s_v", sq)], semkey="st_vs")
                rmsnorm(PC_BNORM, reuse=True)
                for blk in range(2):
                    ws = load_w(s_wq[blk], 4096, "m_wq")
                    for m in range(4):
                        j = blk * 4 + m
                        b = newbank()
                        mm_group(b, ws, m * 128, 512, lambda kc: xn[:, kc, :], KC, ["xn"])
                        op("act", lambda e, b=b, j=j: e.copy(out=yb[:, j, :], in_=psum[:, b, :]), reads=[("ps", b)], writes=["yb"])
                dma("pool", tile_cols(fm_view(s_qT[sq]), t), yb, reads=["yb"], commit_only=[("s_qT", sq)], semkey="st_q")
                dma("pool", tile_cols(fm_view(s_h1[sq]), t), h, reads=["h"], commit_only=[("s_h1", sq)], semkey="st_h1")

        mt4 = mtab[:].rearrange("p (d h n) -> p d h n", d=3, h=NH)

        def phase_b2(sq):
            NHF = S // 2048
            for c in range(8):
                dma("sp", kcb[:, 0:S], s_kT[sq, c * 128:(c + 1) * 128, :], reads=[("s_kT", sq)], writes=["kcb"])
                vsrc = s_v[sq, :, c * 128:(c + 1) * 128]
                dma("sp", vc[:, 0, 0:S // 128, :], vsrc.rearrange("(b p) f -> p b f", p=128), reads=[("s_v", sq)], writes=["vc0"])
                for jj in range(S // 512):
                    dma("sp", vc[:, 1, jj * 4:(jj + 1) * 4, :], vsrc[jj * 512:(jj + 1) * 512, :].rearrange("(p r) f -> p r f", r=4),
                        reads=[("s_v", sq)], writes=[("vc1", jj)])
                for jj in range(NHF):
                    for rg in range(4):
                        dma("sp", vc[:, 2, jj * 16 + rg * 4: jj * 16 + rg * 4 + 4, :],
                            vsrc[jj * 2048:(jj + 1) * 2048, :].rearrange("(p r) f -> p r f", r=16)[:, rg * 4:(rg + 1) * 4, :],
                            reads=[("s_v", sq)], writes=[("vc2", jj, rg)])
                vkeys = ["vc0"] + [("vc1", jj) for jj in range(S // 512)] + [("vc2", jj, rg) for jj in range(NHF) for rg in range(4)]
                for hf in range(NHF):
                    dma("sp", qc, s_qT[sq, c * 128:(c + 1) * 128, hf * 2048:(hf + 1) * 2048], reads=[("s_qT", sq)], writes=["qc"])
                    groups = []
                    gi = 0
                    for di, d in enumerate(DILS):
                        for g in range(4):
                            gpar = (di * 4 + g) % 2
                            bn, bd_ = (0, 1) if gpar == 0 else (2, 3)
                            first = {0: True, 1: True}
                            for pr in range(2):
                                blks = []
                                for bi in range(2):
                                    u = g * 4 + pr * 2 + bi
                                    if d == 1:
                                        jb = hf * 16 + u
                                        qsl = slice(u * 128, (u + 1) * 128)
                                        kcur = slice(jb * 128, (jb + 1) * 128)
                                        jp = max(jb - 1, 0)
                                        kprev = slice(jp * 128, (jp + 1) * 128)
                                        vcur, vprev, has_prev = jb, jp, jb >= 1
                                    elif d == 4:
                                        lt, r = u // 4, u % 4
                                        j4 = hf * 4 + lt
                                        qsl = slice(lt * 512 + r, (lt + 1) * 512, 4)
                                        kcur = slice(j4 * 512 + r, (j4 + 1) * 512, 4)
                                        jp = max(j4 - 1, 0)
                                        kprev = slice(jp * 512 + r, (jp + 1) * 512, 4)
                                        vcur, vprev, has_prev = j4 * 4 + r, jp * 4 + r, j4 >= 1
                                    else:
                                        r = u
                                        qsl = slice(r, 2048, 16)
                                        kcur = slice(hf * 2048 + r, (hf + 1) * 2048, 16)
                                        jp = max(hf - 1, 0)
                                        kprev = slice(jp * 2048 + r, (jp + 1) * 2048, 16)
                                        vcur, vprev, has_prev = hf * 16 + r, jp * 16 + r, hf >= 1
                                    blks.append((qsl, kcur, kprev, vcur, vprev, has_prev))
                                spar = gi % 2
                                groups.append(dict(di=di, d=d, g=g, pr=pr, blks=blks, bs=[4 + spar * 2, 5 + spar * 2],
                                                   bn=bn, bd=bd_, first=first, spar=spar))
                                gi += 1

                    def emit_scores(G):
                        bs, di, spar = G["bs"], G["di"], G["spar"]
                        for bi, (qsl, kcur, kprev, vcur, vprev, has_prev) in enumerate(G["blks"]):
                            for kbp, ksl in enumerate((kcur, kprev)):
                                for hd in range(2):
                                    last = (bi == 1 and kbp == 1)
                                    op("pe", lambda e, hd=hd, bi=bi, kbp=kbp, ksl=ksl, qsl=qsl, bs=bs: e.matmul(
                                        psum[:, bs[hd], (bi * 2 + kbp) * 128:(bi * 2 + kbp + 1) * 128],
                                        lhsT=kcb[hd * 64:(hd + 1) * 64, ksl], rhs=qc[hd * 64:(hd + 1) * 64, qsl],
                                        start=True, stop=True),
                                       reads=["kcb", "qc"], writes=[("ps", bs[hd])], signal=last)
                        pts = []
                        for hd in range(2):
                            pt = pT[spar * 2 + hd]
                            pkey = ("pT", spar * 2 + hd)
                            pts.append((pt, pkey))
                            op("act", lambda e, pt=pt, hd=hd, bs=bs: e.activation(out=pt, in_=psum[:, bs[hd], :], func=AF.Exp, scale=0.125),
                               reads=[("ps", bs[hd])], writes=[pkey])
                            hh = c * 2 + hd
                            for bi in range(2):
                                op("dve", lambda e, pt=pt, bi=bi, hh=hh, di=di: e.tensor_tensor(
                                    out=pt[:, bi * 256:(bi + 1) * 256], in0=pt[:, bi * 256:(bi + 1) * 256],
                                    in1=mt4[:, di, hh, :], op=ALU.mult),
                                   reads=[pkey, "mtab"], writes=[pkey])
                        G["pts"] = pts

                    def emit_pv(G):
                        di, d, g, pr, bn, bd_, first, pts = G["di"], G["d"], G["g"], G["pr"], G["bn"], G["bd"], G["first"], G["pts"]
                        for bi, (qsl, kcur, kprev, vcur, vprev, has_prev) in enumerate(G["blks"]):
                            col = (pr * 2 + bi) * 128
                            for hd in range(2):
                                pt, pkey = pts[hd]
                                for kbp, vb in enumerate((vcur, vprev)):
                                    if kbp == 1 and not has_prev:
                                        continue
                                    rhs = pt[:, (bi * 2 + kbp) * 128:(bi * 2 + kbp + 1) * 128]
                                    stf = first[hd]
                                    first[hd] = False
                                    op("pe", lambda e, hd=hd, vb=vb, rhs=rhs, col=col, stf=stf, di=di, bn=bn: e.matmul(
                                        psum[hd * 64:(hd + 1) * 64, bn, col:col + 128],
                                        lhsT=vc[:, di, vb, hd * 64:(hd + 1) * 64], rhs=rhs, start=stf, stop=True,
                                        skip_group_check=True),
                                       reads=[pkey] + vkeys, writes=[("ps", bn)], signal=False)
                                    op("pe", lambda e, hd=hd, rhs=rhs, col=col, stf=stf, bd_=bd_: e.matmul(
                                        psum[hd * 64:(hd + 1) * 64, bd_, col:col + 128],
                                        lhsT=ones_bf[:, 0:64], rhs=rhs, start=stf, stop=True,
                                        skip_group_check=True),
                                       reads=[pkey, "ones"], writes=[("ps", bd_)], signal=True)
                        if pr != 1:
                            return
                        for (bk, acc, akey) in ((bn, accn, "accn"), (bd_, accd, "accd")):
                            if d == 1:
                                oap = acc[:, g * 512:(g + 1) * 512]
                                iap = psum[:, bk, :]
                            elif d == 4:
                                oap = acc[:, g * 512:(g + 1) * 512].rearrange("f (p r) -> f r p", r=4)
                                iap = psum[:, bk, :].rearrange("f (r p) -> f r p", r=4)
                            else:
                                oap = acc.rearrange("f (p r) -> f r p", r=16)[:, g * 4:(g + 1) * 4, :]
                                iap = psum[:, bk, :].rearrange("f (r p) -> f r p", r=4)
                            if d == 1:
                                op("act", lambda e, oap=oap, iap=iap: e.copy(out=oap, in_=iap), reads=[("ps", bk)], writes=[akey])
                            else:
                                op("dve", lambda e, oap=oap, iap=iap: e.tensor_tensor(out=oap, in0=iap, in1=oap, op=ALU.add),
                                   reads=[("ps", bk), akey], writes=[akey])

                    emit_scores(groups[0])
                    for i_, G in enumerate(groups):
                        if i_ + 1 < len(groups):
                            emit_scores(groups[i_ + 1])
                        emit_pv(G)
                    op("dve", lambda e: e.reciprocal(out=rec, in_=accd), reads=["accd"], writes=["rec"])
                    op("dve", lambda e: e.tensor_tensor(out=ato, in0=accn, in1=rec, op=ALU.mult), reads=["accn", "rec"], writes=["ato"])
                    dma("pool", s_at[sq, c * 128:(c + 1) * 128, hf * 2048:(hf + 1) * 2048], ato, reads=["ato"],
                        commit_only=[("s_at", sq)], semkey="st_ato")

        def phase_b3(sq):
            op("dve", lambda e: e.memset(hal_ffn[:], 0.0), writes=["hal_ffn"] + [("hf", 1, c) for c in range(44)])
            for t in range(NT):
                dma("sp", h, tile_cols(fm_view(s_h1[sq]), t), reads=[("s_h1", sq)], writes=["h"])
                dma("sp", yb, tile_cols(fm_view(s_at[sq]), t), reads=[("s_at", sq)], writes=["yb"])
                residual_proj(s_wo, "m_wo", yb, "yb", KC, 4)
                ffn(1, PC_FNORM1)
                rmsnorm(PC_FINAL, out_bf=False)
                dma("pool", tile_cols(fm_view(out_fm[sq]), t), h, reads=["h"], commit_only=["out"], semkey="st_out")

        for sq in range(nseq):
            if stop_after == "tables":
                break
            phase_a(sq)
            Sx.barrier()
            if stop_after == "a":
                break
            phase_b2(sq)
            Sx.barrier()
            if stop_after == "b2":
                break
            phase_b3(sq)
            Sx.barrier()
        Sx.barrier()
        Sx.emit()
    return nc


_NC_CACHE = {}


def kernel(**inputs):
    x = np.asarray(inputs["x"], np.float32)
    B = x.shape[0]
    ncores = 8
    nseq = B // ncores
    if "nc" not in _NC_CACHE:
        _NC_CACHE["nc"] = build_program(nseq=nseq)
    nc = _NC_CACHE["nc"]
    onehot, band, jm = _consts()
    prm = _pack_params(inputs)
    f32 = lambda a: np.ascontiguousarray(np.asarray(a, np.float32))
    shared = {
        "a_w_in": f32(inputs["a_w_in"][0]),
        "a_w_out": f32(inputs["a_w_out"][0]),
        "w_kv": f32(inputs["w_kv"]),
        "b_w_q": f32(inputs["b_w_q"][0]),
        "b_w_o": f32(inputs["b_w_o"][0]),
        "ffn_w_up0": f32(inputs["ffn_w_up"][0]),
        "ffn_w_up1": f32(inputs["ffn_w_up"][1]),
        "ffn_w_dn0": f32(inputs["ffn_w_down"][0]),
        "ffn_w_dn1": f32(inputs["ffn_w_down"][1]),
        "rel_bias": f32(inputs["rel_bias"]),
        "prm": prm, "onehot": onehot, "band": band, "jm": jm,
    }
    in_maps = []
    for c in range(ncores):
        m = dict(shared)
        m["x_fm"] = np.ascontiguousarray(x[c * nseq:(c + 1) * nseq].transpose(0, 2, 1))
        in_maps.append(m)
    res = run_bass_kernel_spmd(nc, in_maps, core_ids=list(range(ncores)))
    outs = [np.asarray(r["out_fm"]).transpose(0, 2, 1) for r in res.results]
    return np.ascontiguousarray(np.concatenate(outs, axis=0).astype(np.float32))
```
